# Optimizing a Trainium2 kernel written in Bass

```python
import jax, jax.numpy as jnp
from jax import lax
import numpy as np

D_MODEL = 2048
BATCH = 8
SEQ = 2048
DEPTH = 2

N_MIXERS = 2
N_MLSTM_LAYERS = (DEPTH + 1) // 2
N_ATTN_LAYERS = DEPTH // 2
N_MOD = 6

M_HEADS = 8
M_DV = D_MODEL // M_HEADS
M_DQK = M_DV // 2
M_CHUNK = 64
M_GATE_CAP = 15.0
M_QK_W = M_HEADS * M_DQK
M_V_W = M_HEADS * M_DV
M_IN_W = 2 * M_QK_W + 2 * M_V_W + 2 * M_HEADS

A_HEAD_DIM = 128
A_HEADS = D_MODEL // A_HEAD_DIM
A_GROUPS = ((128, 1), (512, 4), (2048, 16))
A_N_GROUPS = len(A_GROUPS)
A_BLOCK = 128
A_WIDTH = A_HEADS * A_HEAD_DIM
A_IN_W = A_N_GROUPS * 3 * A_WIDTH
ROPE_THETA = 500000.0
ROPE_DIM = A_HEAD_DIM // 4

D_FF = 5632
CONV_WIDTH = 3
EPS = 1e-6

kernel_name = "hybrid_mlstm_dilated_attn_convffn_adaln"


def rms_norm(x, g):
    xf = x.astype(jnp.float32)
    y = xf * lax.rsqrt(jnp.mean(xf * xf, axis=-1, keepdims=True) + EPS)
    return (y * g.astype(jnp.float32)).astype(x.dtype)


def partial_rope(t, positions):
    half = ROPE_DIM // 2
    inv_freq = ROPE_THETA ** (-jnp.arange(half, dtype=jnp.float32) / half)
    ang = positions.astype(jnp.float32)[..., None, None] * inv_freq
    cos, sin = jnp.cos(ang), jnp.sin(ang)
    t1, t2, rest = t[..., :half], t[..., half:ROPE_DIM], t[..., ROPE_DIM:]
    return jnp.concatenate([t1 * cos - t2 * sin, t2 * cos + t1 * sin, rest], axis=-1)


def _chunk_major(t, n_chunks):
    B, S, H = t.shape[:3]
    t = t.reshape((B, n_chunks, M_CHUNK, H) + t.shape[3:])
    return t.transpose((1, 0, 3, 2) + tuple(range(4, t.ndim)))


def mlstm_chunkwise(q, k, v, log_i, log_f):
    B, S, H, _ = q.shape
    n_chunks = S // M_CHUNK
    xs = tuple(_chunk_major(t, n_chunks) for t in (q, k, v, log_i, log_f))
    causal = jnp.tril(jnp.ones((M_CHUNK, M_CHUNK), dtype=bool))

    def step(carry, inp):
        C, n, m = carry
        qc, kc, vc, ic, fc = inp
        b = jnp.cumsum(fc, axis=-1)
        dmat = b[..., :, None] - b[..., None, :] + ic[..., None, :]
        dmat = jnp.where(causal, dmat, -jnp.inf)
        inter = b + m[..., None]
        m_s = jnp.maximum(inter, jnp.max(dmat, axis=-1))
        a_inter = jnp.exp(inter - m_s)
        w_intra = jnp.einsum('bhsd,bhud->bhsu', qc, kc) * jnp.exp(dmat - m_s[..., None])
        num = (jnp.einsum('bhsu,bhue->bhse', w_intra, vc)
               + a_inter[..., None] * jnp.einsum('bhsd,bhed->bhse', qc, C))
        den = jnp.sum(w_intra, axis=-1) + a_inter * jnp.einsum('bhsd,bhd->bhs', qc, n)
        hc = num / jnp.maximum(jnp.abs(den), jnp.exp(-m_s))[..., None]
        b_last = b[..., -1]
        g = b_last[..., None] - b + ic
        m_new = jnp.maximum(b_last + m, jnp.max(g, axis=-1))
        wk = jnp.exp(g - m_new[..., None])
        decay = jnp.exp(b_last + m - m_new)
        C_new = decay[..., None, None] * C + jnp.einsum('bhu,bhue,bhud->bhed', wk, vc, kc)
        n_new = decay[..., None] * n + jnp.einsum('bhu,bhud->bhd', wk, kc)
        return (C_new, n_new, m_new), hc

    init = (jnp.zeros((B, H, M_DV, M_DQK), jnp.float32),
            jnp.zeros((B, H, M_DQK), jnp.float32),
            jnp.zeros((B, H), jnp.float32))
    _, hs = lax.scan(step, init, xs)
    return hs.transpose(1, 0, 3, 2, 4).reshape(B, S, H, M_DV)


def mlstm_mixer(h, w_in, gate_b, head_g, w_out):
    B, S, _ = h.shape
    proj = (h @ w_in).astype(jnp.float32)
    cuts = [M_QK_W, 2 * M_QK_W, 2 * M_QK_W + M_V_W, 2 * M_QK_W + 2 * M_V_W,
            2 * M_QK_W + 2 * M_V_W + M_HEADS]
    q, k, v, o, ig, fg = jnp.split(proj, cuts, axis=-1)
    q = q.reshape(B, S, M_HEADS, M_DQK) * (M_DQK ** -0.5)
    k = k.reshape(B, S, M_HEADS, M_DQK)
    v = v.reshape(B, S, M_HEADS, M_DV)
    gb = gate_b.astype(jnp.float32)
    log_i = M_GATE_CAP * jnp.tanh((ig + gb[:M_HEADS]) / M_GATE_CAP)
    log_f = jax.nn.log_sigmoid(M_GATE_CAP * jnp.tanh((fg + gb[M_HEADS:]) / M_GATE_CAP))
    ht = mlstm_chunkwise(q, k, v, log_i, log_f)
    ht = ht * lax.rsqrt(jnp.mean(ht * ht, axis=-1, keepdims=True) + EPS)
    ht = ht * head_g.astype(jnp.float32).reshape(M_HEADS, M_DV)
    y = jax.nn.sigmoid(o) * ht.reshape(B, S, M_V_W)
    return y.astype(h.dtype) @ w_out


def dilated_band_attention(q, k, v, window, dilation):
    B, S, H, dh = q.shape
    L = S // dilation
    span = window // dilation
    Q = A_BLOCK
    nb = -(-L // Q)
    pad = nb * Q - L

    def to_sub(t):
        return t.reshape(B, L, dilation, H, dh).transpose(0, 2, 1, 3, 4)

    qs = jnp.pad(to_sub(q), ((0, 0), (0, 0), (0, pad), (0, 0), (0, 0))).reshape(B, dilation, nb, Q, H, dh)

    def key_blocks(t):
        tp = jnp.pad(to_sub(t), ((0, 0), (0, 0), (Q, pad), (0, 0), (0, 0))).reshape(B, dilation, nb + 1, Q, H, dh)
        return jnp.concatenate([tp[:, :, :-1], tp[:, :, 1:]], axis=3)

    ks, vs = key_blocks(k), key_blocks(v)
    s = jnp.einsum('bgnqhd,bgnkhd->bgnhqk', qs, ks) * (dh ** -0.5)
    qi = jnp.arange(Q)[:, None] + Q
    kj = jnp.arange(2 * Q)[None, :]
    dist = qi - kj
    abs_k = jnp.arange(nb)[:, None, None] * Q + kj[None] - Q
    mask = (dist >= 0) & (dist <= span) & (abs_k >= 0)
    s = jnp.where(mask[None, None, :, None], s, -jnp.inf)
    lse = jax.nn.logsumexp(s, axis=-1)
    p = jnp.exp(s - lse[..., None])
    o = jnp.einsum('bgnhqk,bgnkhd->bgnqhd', p, vs)
    o = o.reshape(B, dilation, nb * Q, H, dh)[:, :, :L].transpose(0, 2, 1, 3, 4).reshape(B, S, H, dh)
    lse = lse.transpose(0, 1, 2, 4, 3).reshape(B, dilation, nb * Q, H)[:, :, :L]
    lse = lse.transpose(0, 2, 1, 3).reshape(B, S, H)
    return o, lse


def dilated_mixer(h, positions, w_in, w_out):
    B, S, _ = h.shape
    proj = (h @ w_in).astype(jnp.float32).reshape(B, S, A_N_GROUPS, 3, A_HEADS, A_HEAD_DIM)
    outs, lses = [], []
    for g, (window, dilation) in enumerate(A_GROUPS):
        q = partial_rope(proj[:, :, g, 0], positions)
        k = partial_rope(proj[:, :, g, 1], positions)
        o, lse = dilated_band_attention(q, k, proj[:, :, g, 2], window, dilation)
        outs.append(o)
        lses.append(lse)
    alpha = jax.nn.softmax(jnp.stack(lses), axis=0)
    o = jnp.sum(alpha[..., None] * jnp.stack(outs), axis=0)
    return o.reshape(B, S, A_WIDTH).astype(h.dtype) @ w_out


def conv_ffn(h, w_up, conv_w, conv_b, w_down):
    u = h @ w_up
    u = lax.conv_general_dilated(u, conv_w[:, None, :], window_strides=(1,),
                                 padding=[(CONV_WIDTH - 1, 0)],
                                 dimension_numbers=('NWC', 'WIO', 'NWC'),
                                 feature_group_count=u.shape[-1]) + conv_b
    gate, val = jnp.split(u, 2, axis=-1)
    return (jax.nn.silu(gate) * val) @ w_down


def setup_inputs(seed: int = 0) -> dict:
    key = jax.random.key(seed)
    ks = jax.random.split(key, 20)

    def nrm(k, shape, scale):
        return jax.random.normal(k, shape, jnp.float32) * scale

    x = nrm(ks[0], (BATCH, SEQ, D_MODEL), 1.0)
    c = nrm(ks[1], (BATCH, D_MODEL), 1.0)
    positions = (jnp.arange(SEQ, dtype=jnp.int32)[None, :]
                 + jax.random.randint(ks[2], (BATCH, 1), 0, 1024, dtype=jnp.int32))
    w_ada = nrm(ks[3], (D_MODEL, DEPTH * N_MOD * D_MODEL), 0.5 * D_MODEL ** -0.5)
    b_ada = nrm(ks[4], (DEPTH * N_MOD * D_MODEL,), 0.02)
    norm_mix = 1.0 + nrm(ks[5], (DEPTH, D_MODEL), 0.05)
    norm_ffn = 1.0 + nrm(ks[6], (DEPTH, D_MODEL), 0.05)
    norm_out = 1.0 + nrm(ks[7], (D_MODEL,), 0.05)
    m_w_in = nrm(ks[8], (N_MLSTM_LAYERS, D_MODEL, M_IN_W), D_MODEL ** -0.5)
    kb1, kb2 = jax.random.split(ks[9])
    i_bias = nrm(kb1, (N_MLSTM_LAYERS, M_HEADS), 0.1)
    f_bias = jnp.linspace(3.0, 6.0, M_HEADS, dtype=jnp.float32)[None, :] + nrm(kb2, (N_MLSTM_LAYERS, M_HEADS), 0.1)
    m_gate_b = jnp.concatenate([i_bias, f_bias], axis=-1)
    m_head_norm = 1.0 + nrm(ks[10], (N_MLSTM_LAYERS, M_V_W), 0.05)
    m_w_out = nrm(ks[11], (N_MLSTM_LAYERS, M_V_W, D_MODEL), M_V_W ** -0.5)
    a_w_in = nrm(ks[12], (N_ATTN_LAYERS, D_MODEL, A_IN_W), D_MODEL ** -0.5)
    a_w_out = nrm(ks[13], (N_ATTN_LAYERS, A_WIDTH, D_MODEL), A_WIDTH ** -0.5)
    f_w_up = nrm(ks[14], (DEPTH, D_MODEL, 2 * D_FF), D_MODEL ** -0.5)
    f_conv_w = nrm(ks[15], (DEPTH, CONV_WIDTH, 2 * D_FF), CONV_WIDTH ** -0.5)
    f_conv_b = nrm(ks[16], (DEPTH, 2 * D_FF), 0.02)
    f_w_down = nrm(ks[17], (DEPTH, D_FF, D_MODEL), D_FF ** -0.5)
    return {"x": x, "c": c, "positions": positions, "w_ada": w_ada, "b_ada": b_ada,
            "norm_mix": norm_mix, "norm_ffn": norm_ffn, "norm_out": norm_out,
            "m_w_in": m_w_in, "m_gate_b": m_gate_b, "m_head_norm": m_head_norm, "m_w_out": m_w_out,
            "a_w_in": a_w_in, "a_w_out": a_w_out,
            "f_w_up": f_w_up, "f_conv_w": f_conv_w, "f_conv_b": f_conv_b, "f_w_down": f_w_down}


def reference(x, c, positions, w_ada, b_ada, norm_mix, norm_ffn, norm_out,
              m_w_in, m_gate_b, m_head_norm, m_w_out, a_w_in, a_w_out,
              f_w_up, f_conv_w, f_conv_b, f_w_down):
    B = x.shape[0]
    mod = (jax.nn.silu(c) @ w_ada + b_ada).reshape(B, DEPTH, N_MOD, 1, D_MODEL)
    for i in range(DEPTH):
        sh_m, sc_m, g_m, sh_f, sc_f, g_f = (mod[:, i, j] for j in range(N_MOD))
        h = rms_norm(x, norm_mix[i]) * (1.0 + sc_m) + sh_m
        j = i // N_MIXERS
        if i % N_MIXERS == 0:
            y = mlstm_mixer(h, m_w_in[j], m_gate_b[j], m_head_norm[j], m_w_out[j])
        else:
            y = dilated_mixer(h, positions, a_w_in[j], a_w_out[j])
        x = x + g_m * y
        h = rms_norm(x, norm_ffn[i]) * (1.0 + sc_f) + sh_f
        x = x + g_f * conv_ffn(h, f_w_up[i], f_conv_w[i], f_conv_b[i], f_w_down[i])
    return rms_norm(x, norm_out)
```

```python
import math
import numpy as np
import concourse.bass as bass
import concourse.mybir as mybir
from concourse.bass_utils import run_bass_kernel_spmd

F32 = mybir.dt.float32
BF16 = mybir.dt.bfloat16
I32 = mybir.dt.int32
AF = mybir.ActivationFunctionType
ALU = mybir.AluOpType
AX = mybir.AxisListType

S = 2048
D = 2048
NCH = 16
DFF = 5632
NFF = 44
EPS = 1e-6
N_CORES = 8


class Prog:
    ENG = ("pe", "act", "dve", "pool", "sp")

    def __init__(self, nc):
        self.nc = nc
        self.q = {e: [] for e in self.ENG}
        self.cnt = {}
        self.waited = {e: {} for e in self.ENG}
        self.sem_keys = []
        for e in self.ENG:
            self._key("E_" + e)

    def _key(self, k):
        if k not in self.cnt:
            self.cnt[k] = 0
            self.sem_keys.append(k)
        return k

    def _waits(self, eng, deps):
        ws = []
        for d in deps:
            if d is None:
                continue
            if isinstance(d, list):
                ws += self._waits(eng, d)
                continue
            k, v = d
            if self.waited[eng].get(k, 0) < v:
                self.waited[eng][k] = v
                ws.append((k, v))
        return ws

    def op(self, eng, fn, deps=(), sig=True):
        ws = self._waits(eng, deps)
        tok = None
        if sig:
            k = "E_" + eng
            self.cnt[k] += 1
            tok = (k, self.cnt[k])
        self.q[eng].append((ws, fn, ("E_" + eng, 1) if sig else None))
        return tok

    def dma(self, eng, out, in_, semkey, deps=()):
        self._key(semkey)
        ws = self._waits(eng, deps)
        self.cnt[semkey] += 16
        tok = (semkey, self.cnt[semkey])
        self.q[eng].append((ws, lambda e, o=out, i=in_: e.dma_start(out=o, in_=i), (semkey, 16)))
        return tok

    def pe(self, fn, deps=(), sig=False):
        return self.op("pe", fn, deps, sig)

    def act(self, fn, deps=()):
        return self.op("act", fn, deps)

    def dve(self, fn, deps=()):
        return self.op("dve", fn, deps)

    def pool(self, fn, deps=()):
        return self.op("pool", fn, deps)

    def emit(self, final_tokens):
        nc = self.nc
        sems = {}
        import contextlib
        with contextlib.ExitStack() as st:
            for k in self.sem_keys:
                sems[k] = st.enter_context(nc.semaphore(k))
            block = st.enter_context(nc.Block())

            def run(engname):
                def body(e):
                    for ws, fn, inc in self.q[engname]:
                        for (k, v) in ws:
                            e.wait_ge(sems[k], v)
                        ins = fn(e)
                        if inc is not None:
                            ins.then_inc(sems[inc[0]], inc[1])
                    if engname == "sp":
                        for (k, v) in final_tokens:
                            e.wait_ge(sems[k], v)
                return body

            block.tensor(run("pe"))
            block.scalar(run("act"))
            block.vector(run("dve"))
            block.gpsimd(run("pool"))
            block.sync(run("sp"))


class Ring:
    def __init__(self, P, name, tiles):
        self.P = P
        self.name = name
        self.tiles = tiles
        self.n = len(tiles)
        self.ready = [None] * self.n
        self.free = [None] * self.n

    def slot(self, i):
        return i % self.n

    def tile(self, i):
        return self.tiles[i % self.n]

    def load(self, i, eng, in_ap, out_fn=None):
        s = i % self.n
        out = self.tiles[s][:] if out_fn is None else out_fn(self.tiles[s])
        tok = self.P.dma(eng, out, in_ap, f"R_{self.name}_{s}", deps=[self.free[s]])
        self.ready[s] = tok
        self.free[s] = None
        return tok

    def rdy(self, i):
        return self.ready[i % self.n]

    def release(self, i, tok):
        s = i % self.n
        if self.free[s] is None:
            self.free[s] = []
        if not isinstance(self.free[s], list):
            self.free[s] = [self.free[s]]
        self.free[s].append(tok)


class Slots:
    def __init__(self, tiles):
        self.tiles = tiles
        self.n = len(tiles)
        self.free = [[] for _ in tiles]
        self.i = -1

    def next(self):
        self.i += 1
        s = self.i % self.n
        deps = self.free[s]
        self.free[s] = []
        return self.tiles[s], deps, s

    def release(self, s, tok):
        self.free[s].append(tok)


def _fm(v):
    v = np.asarray(v)
    return np.ascontiguousarray(v.reshape(-1, 128).T)


ATT_GROUPS = ((128, 1), (512, 4), (2048, 16))


def unit_tokens(g):
    d = ATT_GROUPS[g][1]
    L = S // d
    nb = L // 128
    tok = np.zeros((16, 128), dtype=np.int64)
    for r in range(d):
        for n in range(nb):
            tok[r * nb + n] = r + d * (128 * n + np.arange(128))
    return tok


def pack_shared(inp):
    sh = {}
    sh["w_ada"] = np.ascontiguousarray(inp["w_ada"], dtype=np.float32)
    sh["b_adaT"] = _fm(inp["b_ada"])
    nrm = np.stack([_fm(inp["norm_mix"][0]), _fm(inp["norm_mix"][1]),
                    _fm(inp["norm_ffn"][0]), _fm(inp["norm_ffn"][1]), _fm(inp["norm_out"])], axis=1)
    sh["nrm"] = np.ascontiguousarray(nrm)
    w = inp["m_w_in"][0]
    wk = w.reshape(16, 128, 6160)
    heads = []
    for h in range(8):
        cols = np.concatenate([np.arange(h * 128, (h + 1) * 128), 1024 + np.arange(h * 128, (h + 1) * 128),
                               2048 + np.arange(h * 256, (h + 1) * 256), 4096 + np.arange(h * 256, (h + 1) * 256)])
        heads.append(wk[:, :, cols].transpose(1, 0, 2))
    sh["m_wh"] = np.ascontiguousarray(np.stack(heads))
    sh["m_wg"] = np.ascontiguousarray(wk[:, :, 6144:6160].transpose(1, 0, 2))
    sh["m_gb"] = np.ascontiguousarray(np.broadcast_to(inp["m_gate_b"][0][None, None, :], (128, 16, 16)).reshape(128, 256))
    sh["m_hg"] = np.ascontiguousarray(np.broadcast_to(inp["m_head_norm"][0][None, :], (128, 2048)))
    sh["m_wo"] = _pack_out(inp["m_w_out"][0], 16)
    wa = inp["a_w_in"][0].reshape(16, 128, 3, 3, 16, 128)
    sh["a_wi"] = np.ascontiguousarray(wa.transpose(4, 2, 1, 0, 3, 5).reshape(16, 3, 128, 16, 384))
    sh["a_wo"] = _pack_out(inp["a_w_out"][0], 16)
    wu = inp["f_w_up"].reshape(2, 16, 128, 2, 44, 128)
    sh["f_wu"] = np.ascontiguousarray(wu.transpose(0, 4, 2, 1, 3, 5).reshape(2, 44, 128, 16, 256))
    cw = inp["f_conv_w"].reshape(2, 3, 88, 128)
    sh["f_cw"] = np.ascontiguousarray(cw.transpose(3, 0, 1, 2))
    cb = inp["f_conv_b"].reshape(2, 88, 128)
    sh["f_cb"] = np.ascontiguousarray(cb.transpose(2, 0, 1))
    sh["f_wd"] = np.ascontiguousarray(np.stack([_pack_out(inp["f_w_down"][l], 44) for l in range(2)]))
    return sh


def _pack_out(w, nk):
    return np.ascontiguousarray(w.reshape(nk, 128, 16, 128).transpose(2, 1, 0, 3))


def pack_core(inp, b):
    pc = {}
    pc["xT"] = np.ascontiguousarray(inp["x"][b].T.reshape(16, 128, S).transpose(1, 0, 2))
    pc["cT"] = _fm(inp["c"][b])
    pos = np.asarray(inp["positions"][b])
    pu = np.stack([pos[unit_tokens(g)].T for g in range(3)], axis=1)
    pc["posu"] = np.ascontiguousarray(pu.astype(np.int32))
    return pc


SHAPES = {
    "w_ada": ([2048, 24576], F32), "b_adaT": ([128, 192], F32), "nrm": ([128, 5, 16], F32),
    "m_wh": ([8, 128, 16, 768], F32), "m_wg": ([128, 16, 16], F32), "m_gb": ([128, 256], F32),
    "m_hg": ([128, 2048], F32), "m_wo": ([16, 128, 16, 128], F32),
    "a_wi": ([16, 3, 128, 16, 384], F32), "a_wo": ([16, 128, 16, 128], F32),
    "f_wu": ([2, 44, 128, 16, 256], F32), "f_cw": ([128, 2, 3, 88], F32), "f_cb": ([128, 2, 88], F32),
    "f_wd": ([2, 16, 128, 44, 128], F32),
    "xT": ([128, 16, S], F32), "cT": ([128, 16], F32), "posu": ([128, 3, 16], I32),
}


class Ctx:
    pass


_UID = [0]


def _uid():
    _UID[0] += 1
    return "u%d_" % _UID[0]


def _mm(out, lhsT, rhs, start, stop):
    return lambda e: e.matmul(out, lhsT=lhsT, rhs=rhs, start=start, stop=stop)


def build_program(upto="all", dbg=()):
    import contextlib
    nc = bass.Bass("TRN2", target_bir_lowering=False)
    P = Prog(nc)
    T = {}
    for k, (shp, dt_) in SHAPES.items():
        T[k] = nc.dram_tensor(k, shp, dt_, kind="ExternalInput").ap()
    T["outT"] = nc.dram_tensor("outT", [128, 16, S], F32, kind="ExternalOutput").ap()
    T["xa"] = nc.dram_tensor("xa", [128, 16, S], F32, kind="Internal").ap()
    T["xb"] = nc.dram_tensor("xb", [128, 16, S], F32, kind="Internal").ap()
    T["yTd"] = nc.dram_tensor("yTd", [128, 16, S], BF16, kind="Internal").ap()
    DBG = {}
    c = Ctx()
    c.nc, c.P, c.T, c.dbg, c.DBG = nc, P, T, dbg, DBG
    c.final = []

    with contextlib.ExitStack() as gst:
        def gsb(name, shape, dt_):
            return gst.enter_context(nc.sbuf_tensor("g_" + name, shape, dt_))
        c.banks = [gst.enter_context(nc.psum_tensor(f"bank{i}", [128, 512], F32)) for i in range(8)]
        c.modT = gsb("modT", [128, 192], F32)
        c.gm = gsb("gm", [128, 4, 16], F32)
        c.nrm = gsb("nrm", [128, 5, 16], F32)
        c.ones_m = gsb("ones_m", [128, 128], F32)
        c.ones32 = gsb("ones32", [128, 128], F32)
        c.onesb = gsb("onesb", [128, 128], BF16)
        c.identb = gsb("identb", [128, 128], BF16)
        c.tri = gsb("tri", [128, 128], F32)
        c.tri_ge = gsb("tri_ge", [128, 128], F32)
        c.eps_t = gsb("eps_t", [128, 1], F32)
        c.cs = gsb("cs", [128, 16], BF16)
        c.badaT = gsb("badaT", [128, 192], F32)
        c.one11 = gsb("one11", [1, 1], F32)
        c.hT = gsb("hT", [128, 16, S], BF16)
        consts_init(c)
        c.last = {}
        phase_ada(c)
        barrier(c)
        phase_norm(c, T["xT"], li=0, out_h=True)
        barrier(c)
        if "hT" in dbg:
            dbg_dump(c, "hT", c.hT, [128, 16, S], BF16)
        if upto == "norm0":
            return finish(c)
        phase_mlstm(c)
        barrier(c)
        if upto == "mlstm":
            dbg_dump(c, "yT", None, None, None) if False else None
            return finish(c)
        phase_outproj(c, T["m_wo"], 16, mod_col(c, 0, 2), T["xT"], T["xa"], next_li=2)
        barrier(c)
        if "xa" in dbg:
            dbg_dram(c, "xa", T["xa"], [128, 16, S], F32)
        if upto == "mix0":
            return finish(c)
        phase_ffn(c, 0, T["xa"], T["xb"], next_li=1)
        barrier(c)
        if "xb" in dbg:
            dbg_dram(c, "xb", T["xb"], [128, 16, S], F32)
        if upto == "ffn0":
            return finish(c)
        phase_attn(c)
        barrier(c)
        phase_outproj(c, T["a_wo"], 16, mod_col(c, 1, 2), T["xb"], T["xa"], next_li=3)
        barrier(c)
        if "xa2" in dbg:
            dbg_dram(c, "xa2", T["xa"], [128, 16, S], F32)
        if upto == "mix1":
            return finish(c)
        phase_ffn(c, 1, T["xa"], None, next_li=4, out_dram=T["outT"])
        return finish(c)


def finish(c):
    c.P.emit(c.final)
    return c.nc


def dbg_dump(c, name, sb_ap, shape, dt_):
    d = c.nc.dram_tensor("dbg_" + name, shape, dt_, kind="ExternalOutput").ap()
    tok = c.P.dma("sp", d, sb_ap[:] if hasattr(sb_ap, "shape") else sb_ap, "DBG_" + name, deps=c.bar)
    c.final.append(tok)
    c.DBG[name] = (shape, dt_)


def dbg_dram(c, name, src, shape, dt_):
    d = c.nc.dram_tensor("dbg_" + name, shape, dt_, kind="ExternalOutput").ap()
    tok = c.P.dma("sp", d, src, "DBG_" + name, deps=c.bar)
    c.final.append(tok)


def barrier(c):
    P = c.P
    toks = [(k, v) for k, v in P.cnt.items() if v > 0]
    c.bar = toks
    for e in P.ENG:
        P.op(e, _nop, deps=toks, sig=(e != "sp"))


def _nop(e):
    return e.nop()


def consts_init(c):
    P = c.P
    P.pool(lambda e: e.memset(c.ones_m[:], 1.0 / D))
    P.pool(lambda e: e.memset(c.eps_t[:], EPS))
    P.pool(lambda e: e.memset(c.ones32[:], 1.0))
    P.pool(lambda e: e.memset(c.onesb[:], 1.0))
    t1 = P.pool(lambda e: e.memset(c.tri[:], 1.0))
    t2 = P.pool(lambda e: e.memset(c.tri_ge[:], 1.0))
    t1 = P.pool(lambda e: e.affine_select(out=c.tri[:], in_=c.tri[:], pattern=[[1, 128]], compare_op=ALU.is_ge,
                                          fill=0.0, base=0, channel_multiplier=-1), deps=[t1])
    t2 = P.pool(lambda e: e.affine_select(out=c.tri_ge[:], in_=c.tri_ge[:], pattern=[[-1, 128]], compare_op=ALU.is_ge,
                                          fill=0.0, base=0, channel_multiplier=1), deps=[t2])
    c.tok_consts = P.pool(lambda e: e.tensor_tensor(out=c.identb[:], in0=c.tri[:], in1=c.tri_ge[:], op=ALU.mult),
                          deps=[t1, t2])
    c.tok_nrm = P.dma("sp", c.nrm[:], c.T["nrm"], "LD_nrm")


N_ADA_UP = 8


def phase_ada(c):
    import contextlib
    nc, P, T = c.nc, c.P, c.T
    with contextlib.ExitStack() as st:
        _u = _uid()

        def sb(name, shape, dt_):
            return st.enter_context(nc.sbuf_tensor(_u + name, shape, dt_))
        cs32 = sb("a_cs32", [128, 16], F32)
        cs, badaT, one11 = c.cs, c.badaT, c.one11
        mr = Slots([sb(f"a_modrow{i}", [1, 512], F32) for i in range(2)])
        wa = Ring(P, "wa", [sb(f"a_wa{i}", [128, 16, 512], BF16) for i in range(3)])
        t_c = P.dma("sp", cs32[:], T["cT"], "LD_c")
        c.t_bada = P.dma("sp", badaT[:], T["b_adaT"], "LD_bada")
        c.t_one = P.pool(lambda e: e.memset(one11[:], 1.0))
        c.t_cs = P.act(lambda e: e.activation(out=cs[:], in_=cs32[:], func=AF.Silu), deps=[t_c])
        t_b, t_one, t_cs = c.t_bada, c.t_one, c.t_cs
        wv = T["w_ada"].rearrange("(kc p) n -> p kc n", p=128)
        NT = N_ADA_UP
        ps = Slots([c.banks[0], c.banks[1]])
        pT = c.banks[2]
        for i in range(min(3, NT)):
            wa.load(i, "pool", wv[:, :, i * 512:(i + 1) * 512])
        for ct in range(NT):
            pt, pdeps, pslot = ps.next()
            w = wa.tile(ct)
            for kc in range(16):
                tk = P.pe(_mm(pt[0:1, :], cs[:, kc:kc + 1], w[:, kc, :], kc == 0, kc == 15),
                          deps=[wa.rdy(ct), t_cs] + pdeps if kc == 0 else (), sig=(kc == 15))
            wa.release(ct, tk)
            if ct + 3 < NT:
                wa.load(ct + 3, "pool", wv[:, :, (ct + 3) * 512:(ct + 4) * 512])
            mrow, mdeps, mslot = mr.next()
            t_ev = P.act(lambda e, pt=pt, mrow=mrow: e.activation(out=mrow[0:1, :], in_=pt[0:1, :], func=AF.Copy),
                         deps=[tk] + mdeps)
            ps.release(pslot, t_ev)
            for j in range(4):
                tk = P.pe(_mm(pT[:, ct * 4 + j:ct * 4 + j + 1], mrow[0:1, j * 128:(j + 1) * 128], one11[0:1, 0:1], True, True),
                          deps=[t_ev, t_one] if j == 0 else (), sig=(j == 3))
            mr.release(mslot, tk)
        nc_ = NT * 4
        t_mod = P.dve(lambda e: e.tensor_tensor(out=c.modT[:, 0:nc_], in0=pT[:, 0:nc_], in1=badaT[:, 0:nc_], op=ALU.add),
                      deps=[tk, t_b])
        P.dve(lambda e: e.scalar_tensor_tensor(out=c.gm[:, 0, :], in0=c.modT[:, 16:32], scalar=1.0,
                                               in1=c.nrm[:, 0, :], op0=ALU.add, op1=ALU.mult), deps=[t_mod, c.tok_nrm])


def ada_background(c, sb, bank, bank_fr, nring=3):
    P, T = c.P, c.T
    cs, one11 = c.cs, c.one11
    wv = T["w_ada"].rearrange("(kc p) n -> p kc n", p=128)
    c0 = N_ADA_UP * 512
    NTB = (24576 - c0) // 512
    wa = Ring(P, "wab", [sb(f"wab{i}", [128, 16, 512], BF16) for i in range(nring)])
    mr = Slots([sb(f"mrb{i}", [1, 512], F32) for i in range(2)])
    raw = sb("modraw", [128, 192], F32)
    for i in range(nring):
        wa.load(i, "pool", wv[:, :, c0 + i * 512:c0 + (i + 1) * 512])
    last = None
    pendT = None

    def transposes(ct, mrow, mslot, t_ev):
        for j in range(4):
            tk = P.pe(_mm(bank[:, j:j + 1], mrow[0:1, j * 128:(j + 1) * 128], one11[0:1, 0:1], True, True),
                      deps=[t_ev, c.t_one] + (bank_fr.take() if j == 0 else []) if j == 0 else (), sig=(j == 3))
        mr.release(mslot, tk)
        col = N_ADA_UP * 4 + ct * 4
        lt = P.act(lambda e, col=col: e.activation(out=raw[:, col:col + 4], in_=bank[:, 0:4], func=AF.Copy), deps=[tk])
        bank_fr.add(lt)
        return lt

    for ct in range(NTB):
        if pendT is not None:
            last = transposes(*pendT)
            pendT = None
        w = wa.tile(ct)
        for hf in range(2):
            for kc in range(16):
                tk = P.pe(_mm(bank[0:1, hf * 256:(hf + 1) * 256], cs[:, kc:kc + 1], w[:, kc, hf * 256:(hf + 1) * 256], kc == 0, kc == 15),
                          deps=[wa.rdy(ct), c.t_cs] + bank_fr.take() if (kc == 0 and hf == 0) else (), sig=(kc == 15 and hf == 1))
        wa.release(ct, tk)
        if ct + nring < NTB:
            wa.load(ct + nring, "pool", wv[:, :, c0 + (ct + nring) * 512:c0 + (ct + nring + 1) * 512])
        mrow, mdeps, mslot = mr.next()
        t_ev = P.act(lambda e, mrow=mrow: e.activation(out=mrow[0:1, :], in_=bank[0:1, 0:512], func=AF.Copy), deps=[tk] + mdeps)
        bank_fr.add(t_ev)
        pendT = (ct, mrow, mslot, t_ev)
        yield
    last = transposes(*pendT)
    n0 = N_ADA_UP * 4
    t_mod = P.dve(lambda e: e.tensor_tensor(out=c.modT[:, n0:192], in0=raw[:, n0:192], in1=c.badaT[:, n0:192], op=ALU.add),
                  deps=[last, c.t_bada])
    for (gi, jn, jm) in ((1, 2, 4), (2, 1, 7), (3, 3, 10)):
        P.dve(lambda e, gi=gi, jn=jn, jm=jm: e.scalar_tensor_tensor(
            out=c.gm[:, gi, :], in0=c.modT[:, jm * 16:(jm + 1) * 16], scalar=1.0,
            in1=c.nrm[:, jn, :], op0=ALU.add, op1=ALU.mult), deps=[t_mod, c.tok_nrm])
    yield


def mod_col(c, l, j):
    k = (l * 6 + j) * 16
    return c.modT[:, k:k + 16]


def phase_norm(c, x_src, li, out_h=True, out_dram=None):
    import contextlib
    nc, P, T = c.nc, c.P, c.T
    with contextlib.ExitStack() as st:
        _u = _uid()

        def sb(name, shape, dt_):
            return st.enter_context(nc.sbuf_tensor(_u + name, shape, dt_))
        xr = Ring(P, "nx", [sb(f"n_x{i}", [128, 16, 512], F32) for i in range(2)])
        sq = Slots([sb(f"n_sq{i}", [128, 512], F32) for i in range(3)])
        tmp = Slots([sb(f"n_t{i}", [128, 512], F32) for i in range(3)])
        rstd = Slots([sb(f"n_r{i}", [128, 512], F32) for i in range(2)])
        ot = None
        ps = Slots([c.banks[0], c.banks[1]])
        if li < 4:
            l, s_ = (li, 0) if li < 2 else (li - 2, 1)
            gvec = c.gm[:, l * 2 + s_, :]
            shv = mod_col(c, l, 0 if s_ == 0 else 3)
        else:
            gvec = c.nrm[:, 4, :]
            shv = None
        NTT = S // 512
        xr.load(0, "sp", x_src[:, :, 0:512])
        for tt in range(NTT):
            if tt + 1 < NTT:
                xr.load(tt + 1, "sp", x_src[:, :, (tt + 1) * 512:(tt + 2) * 512])
            xt = xr.tile(tt)
            pt, pdeps, pslot = ps.next()
            for fc in range(16):
                sqt, sdeps, sslot = sq.next()
                t_sq = P.act(lambda e, sqt=sqt, xt=xt, fc=fc: e.activation(out=sqt[:], in_=xt[:, fc, :], func=AF.Square),
                             deps=[xr.rdy(tt)] + sdeps)
                tk = P.pe(_mm(pt[:], c.ones_m[:], sqt[:], fc == 0, fc == 15),
                          deps=[t_sq] + (pdeps if fc == 0 else []), sig=True)
                sq.release(sslot, tk)
            rt, rdeps, rslot = rstd.next()
            t_r0 = P.act(lambda e, rt=rt, pt=pt: e.activation(out=rt[:], in_=pt[:], func=AF.Sqrt, bias=c.eps_t[:, 0:1], scale=1.0),
                         deps=[tk] + rdeps)
            ps.release(pslot, t_r0)
            t_r = P.dve(lambda e, rt=rt: e.reciprocal(out=rt[:], in_=rt[:]), deps=[t_r0])
            if not out_h:
                otile = xt
                odeps = []
                os_ = tt % 2
            for fc in range(16):
                if out_h:
                    tm, tdeps, tslot = tmp.next()
                    t_m = P.dve(lambda e, tm=tm, xt=xt, fc=fc, rt=rt: e.scalar_tensor_tensor(
                        out=tm[:], in0=xt[:, fc, :], scalar=gvec[:, fc:fc + 1], in1=rt[:], op0=ALU.mult, op1=ALU.mult),
                        deps=[t_r] + tdeps)
                    t_o = P.act(lambda e, tm=tm, fc=fc, tt=tt: e.activation(
                        out=c.hT[:, fc, tt * 512:(tt + 1) * 512], in_=tm[:], func=AF.Identity,
                        bias=shv[:, fc:fc + 1], scale=1.0), deps=[t_m])
                    tmp.release(tslot, t_o)
                    last = t_o
                else:
                    t_m = P.dve(lambda e, xt=xt, fc=fc, rt=rt, otile=otile: e.scalar_tensor_tensor(
                        out=otile[:, fc, :], in0=xt[:, fc, :], scalar=gvec[:, fc:fc + 1], in1=rt[:],
                        op0=ALU.mult, op1=ALU.mult), deps=[t_r] + (odeps if fc == 0 else []))
                    last = t_m
            rstd.release(rslot, last)
            if not out_h:
                tok = P.dma("sp", out_dram[:, :, tt * 512:(tt + 1) * 512], otile[:], f"ST_no_{os_}", deps=[last])
                xr.release(tt, tok)
                c.final.append(tok)
            else:
                xr.release(tt, last)


def phase_mlstm(c):
    import contextlib
    nc, P, T = c.nc, c.P, c.T
    B = c.banks
    with contextlib.ExitStack() as st:
        _u = _uid()

        def sb(name, shape, dt_):
            return st.enter_context(nc.sbuf_tensor(_u + "ml_" + name, shape, dt_))
        wg = sb("wg", [128, 16, 16], BF16)
        gb = sb("gb", [128, 256], F32)
        hg = sb("hg", [128, 2048], F32)
        t_wg = P.dma("pool", wg[:], T["m_wg"], "LD_mwg")
        t_gb = P.dma("sp", gb[:], T["m_gb"], "LD_mgb")
        t_hg = P.dma("sp", hg[:], T["m_hg"], "LD_mhg")
        wm = Ring(P, "wm", [sb(f"wm{i}", [128, 16, 768], BF16) for i in range(2)])
        wm.load(0, "pool", T["m_wh"][0])
        G = sb("G", [128, 16, 16], F32)
        Tn = sb("Tn", [128, 16, 16], F32)
        Lf = sb("Lf", [128, 16, 8], F32)
        tmpk = sb("tmpk", [128, 16, 8], F32)
        eq = sb("eq", [128, 16, 8], F32)
        ek = sb("ek", [128, 16, 8], F32)
        edec = sb("edec", [128, 16, 8], F32)
        lnq = sb("lnq", [128, 1], F32)
        t_lnq = P.pool(lambda e: e.memset(lnq[:], -0.5 * math.log(128.0)))
        gps = B[0]
        for tt in range(16):
            for kc in range(16):
                tk = P.pe(_mm(gps[:, tt * 16:(tt + 1) * 16], c.hT[:, kc, tt * 128:(tt + 1) * 128], wg[:, kc, :], kc == 0, kc == 15),
                          deps=[t_wg] if (tt == 0 and kc == 0) else (), sig=(tt == 15 and kc == 15))
        gps3 = gps[:, 0:256].rearrange("p (t g) -> p t g", g=16)
        gb3 = gb[:].rearrange("p (t g) -> p t g", g=16)
        t = P.dve(lambda e: e.tensor_tensor(out=G[:], in0=gps3, in1=gb3, op=ALU.add), deps=[tk, t_gb])
        t_tn = P.act(lambda e: e.activation(out=Tn[:], in_=G[:], func=AF.Tanh, scale=1.0 / 15.0), deps=[t])
        t = P.act(lambda e: e.activation(out=Lf[:], in_=Tn[:, :, 8:16], func=AF.Exp, scale=-15.0), deps=[t_tn])
        t_L = P.act(lambda e: e.activation(out=Lf[:], in_=Lf[:], func=AF.Ln, bias=1.0, scale=1.0), deps=[t])
        cps = B[1][:, 0:128].rearrange("p (t g) -> p t g", g=8)
        tps = B[1][:, 128:256].rearrange("p (t g) -> p t g", g=8)
        P.pe(lambda e: e.matmul(cps, lhsT=c.tri[:], rhs=Lf[:], start=True, stop=True), deps=[t_L, c.tok_consts, t])
        tk = P.pe(lambda e: e.matmul(tps, lhsT=c.ones32[:], rhs=Lf[:], start=True, stop=True), sig=True)
        t1 = P.act(lambda e: e.activation(out=eq[:], in_=cps, func=AF.Exp, scale=-1.0, bias=lnq[:, 0:1]), deps=[tk, t_lnq])
        t4 = P.act(lambda e: e.activation(out=edec[:], in_=tps, func=AF.Exp, scale=-1.0), deps=[tk])
        t2 = P.dve(lambda e: e.scalar_tensor_tensor(out=tmpk[:], in0=Tn[:, :, 0:8], scalar=15.0, in1=cps,
                                                    op0=ALU.mult, op1=ALU.add), deps=[tk, t_tn, t1, t4])
        t3 = P.act(lambda e: e.activation(out=ek[:], in_=tmpk[:], func=AF.Exp), deps=[t2])
        t_gates = [t1, t3, t4]
        if "gates" in c.dbg:
            c.bar = t_gates
            dbg_dump(c, "eq", eq, [128, 16, 8], F32)
            dbg_dump(c, "ek", ek, [128, 16, 8], F32)
            dbg_dump(c, "edec", edec, [128, 16, 8], F32)

        NB = 2
        qd = [sb(f"qd{i}", [128, 128], BF16) for i in range(NB)]
        kd = [sb(f"kd{i}", [128, 128], BF16) for i in range(NB)]
        V1 = [sb(f"V1{i}", [128, 260], BF16) for i in range(NB)]
        og = [sb(f"og{i}", [128, 256], F32) for i in range(NB)]
        qkT = [sb(f"qkT{i}", [128, 256], BF16) for i in range(NB)]
        WT = [sb(f"WT{i}", [128, 128], BF16) for i in range(NB)]
        junk = [sb(f"junk{i}", [128, 256], F32) for i in range(NB)]
        yb = [sb(f"yb{i}", [128, 256], BF16) for i in range(NB)]
        stt = [sb(f"st{i}", [128, 8], F32) for i in range(NB)]
        CT32 = sb("CT32", [128, 260], F32)
        CTb = [sb(f"CTb{i}", [128, 260], BF16) for i in range(2)]
        _ys0 = sb("ys0", [128, 2, S], BF16)
        ystage = [_ys0, _ys0]
        t_v1 = [P.pool(lambda e, i=i: e.memset(V1[i][:, 256:260], 1.0)) for i in range(NB)]

        class Fr:
            def __init__(self, init=()):
                self.t = list(init)

            def take(self):
                t = self.t
                self.t = []
                return t

            def add(self, *toks):
                self.t += [x for x in toks if x is not None]

        psQK = [B[0][:, 0:384], B[7][:, 0:384]]
        psVO = B[1]
        psQ = B[2][:, 0:256]
        psS = B[2][:, 256:384]
        psO = B[4][:, 0:257]
        psU = B[5][:, 0:257]
        psY = B[6][:, 0:256]
        fQK = [Fr(t_gates), Fr(t_gates)]
        fVO, fQ, fO, fU, fY = Fr(t_gates), Fr(), Fr(), Fr(), Fr()
        fS = fQ
        bg = ada_background(c, sb, B[3], Fr(t_gates))
        bg_state = {"done": False}

        def bg_step():
            if not bg_state["done"]:
                try:
                    next(bg)
                except StopIteration:
                    bg_state["done"] = True
        f_qd, f_kd, f_V1, f_og, f_qkT, f_WT, f_yb = ([Fr(), Fr()] for _ in range(7))
        f_CTb = [Fr(), Fr()]
        f_ct32 = Fr(t_gates)
        _fys = Fr()
        f_ys = [_fys, _fys]
        gate_dep = list(t_gates)
        A = {}

        def A1(h, tt, W, wtok):
            p = tt % 2
            pqk = psQK[p]
            for kc in range(16):
                tk = P.pe(_mm(pqk, c.hT[:, kc, tt * 128:(tt + 1) * 128], W[:, kc, 0:384], kc == 0, kc == 15),
                          deps=[wtok] + fQK[p].take() if kc == 0 else (), sig=(kc == 15))
            t_q = P.act(lambda e: e.activation(out=qd[p][:], in_=pqk[:, 0:128], func=AF.Copy, scale=eq[:, tt, h:h + 1]),
                        deps=[tk] + gate_dep + f_qd[p].take())
            t_k = P.act(lambda e: e.activation(out=kd[p][:], in_=pqk[:, 128:256], func=AF.Copy, scale=ek[:, tt, h:h + 1]),
                        deps=[tk] + gate_dep + f_kd[p].take())
            v1war = f_V1[p].take()
            t_va = P.act(lambda e: e.activation(out=V1[p][:, 0:128], in_=pqk[:, 256:384], func=AF.Copy),
                         deps=[tk, t_v1[p]] + v1war)
            fQK[p].add(t_q, t_k, t_va)
            A[tt] = dict(t_q=t_q, t_k=t_k, t_va=t_va, v1war=v1war)

        def A2(h, tt, W, wtok):
            p = tt % 2
            a = A[tt]
            for kc in range(16):
                tk = P.pe(_mm(psVO[:, 0:384], c.hT[:, kc, tt * 128:(tt + 1) * 128], W[:, kc, 384:768], kc == 0, kc == 15),
                          deps=[wtok] + fVO.take() if kc == 0 else (), sig=(kc == 15))
            t_vb = P.act(lambda e: e.activation(out=V1[p][:, 128:256], in_=psVO[:, 0:128], func=AF.Copy),
                         deps=[tk, t_v1[p]] + a["v1war"])
            t_v = [a["t_va"], t_vb]
            t_o = P.act(lambda e: e.activation(out=og[p][:], in_=psVO[:, 128:384], func=AF.Exp, scale=-1.0), deps=[tk] + f_og[p].take())
            fVO.add(t_vb, t_o)
            t_o1 = P.dve(lambda e: e.tensor_scalar(out=og[p][:], in0=og[p][:], scalar1=1.0, scalar2=None, op0=ALU.add), deps=[t_o])
            t_o2 = P.dve(lambda e: e.reciprocal(out=og[p][:], in_=og[p][:]), deps=[t_o1])
            t_og = P.pool(lambda e: e.tensor_tensor(out=og[p][:], in0=og[p][:], in1=hg[:, h * 256:(h + 1) * 256], op=ALU.mult),
                          deps=[t_o2, t_hg])
            P.pe(_mm(psQ[:, 0:128], qd[p][:], c.identb[:], True, True), deps=[a["t_q"], a["t_k"], c.tok_consts] + fQ.take())
            tk2 = P.pe(_mm(psQ[:, 128:256], kd[p][:], c.identb[:], True, True), sig=True)
            t_qk = P.dve(lambda e: e.tensor_copy(out=qkT[p][:], in_=psQ), deps=[tk2] + f_qkT[p].take())
            fQ.add(t_qk)
            f_qd[p].add(tk2)
            a.update(t_qk=t_qk, t_v=t_v, t_og=t_og)

        state = {"cur": 0, "t_ctb": None}

        def B1(h, tt):
            p = tt % 2
            a = A[tt]
            tk = P.pe(_mm(psS, qkT[p][:, 128:256], qkT[p][:, 0:128], True, True), deps=[a["t_qk"]] + fS.take(), sig=True)
            t_w = P.dve(lambda e: e.tensor_tensor(out=WT[p][:], in0=psS, in1=c.tri[:], op=ALU.mult),
                        deps=[tk, c.tok_consts] + f_WT[p].take())
            fS.add(t_w)
            a["t_w"] = t_w

        def B2(h, tt):
            p = tt % 2
            a = A[tt]
            cur = state["cur"]
            nxt = 1 - cur
            t_w = a["t_w"]
            P.pe(_mm(psO, WT[p][:], V1[p][:, 0:257], True, False), deps=[t_w, a["t_v"]] + fO.take())
            tk_o = P.pe(_mm(psO, qkT[p][:, 0:128], CTb[cur][:, 0:257], False, True), deps=[state["t_ctb"]], sig=True)
            f_WT[p].add(tk_o)
            f_CTb[cur].add(tk_o)
            tk_u = P.pe(_mm(psU, kd[p][:], V1[p][:, 0:257], True, True), deps=[a["t_k"], a["t_v"]] + fU.take(), sig=True)
            f_kd[p].add(tk_u)
            f_V1[p].add(tk_o, tk_u)
            f_qkT[p].add(tk_o)
            tp_ = max(tt - 1, 0)
            t_s2 = P.dve(lambda e: e.scalar_tensor_tensor(out=CT32[:, 0:257], in0=CT32[:, 0:257], scalar=edec[:, tp_, h:h + 1], in1=psU,
                                                          op0=ALU.mult, op1=ALU.add), deps=[tk_u] + f_ct32.take())
            fU.add(t_s2)
            t_cb = P.act(lambda e: e.activation(out=CTb[nxt][:, 0:257], in_=CT32[:, 0:257], func=AF.Copy, scale=edec[:, tt, h:h + 1]),
                         deps=[t_s2] + f_CTb[nxt].take())
            f_ct32.add(t_cb, t_s2)
            state["cur"] = nxt
            state["t_ctb"] = t_cb
            s = stt[p]
            t1_ = P.dve(lambda e: e.tensor_scalar(out=s[:, 0:1], in0=psO[:, 256:257], scalar1=-16.0, scalar2=16.0,
                                                  op0=ALU.mult, op1=ALU.max), deps=[tk_o] + f_yb[p].take())
            t2_ = P.dve(lambda e: e.scalar_tensor_tensor(out=s[:, 1:2], in0=psO[:, 256:257], scalar=16.0, in1=s[:, 0:1],
                                                         op0=ALU.mult, op1=ALU.max), deps=[t1_])
            t3_ = P.dve(lambda e: e.reciprocal(out=s[:, 2:3], in_=s[:, 1:2]), deps=[t2_])
            t4_ = P.act(lambda e: e.activation(out=junk[p][:], in_=psO[:, 0:256], func=AF.Square, scale=s[:, 2:3],
                                               accum_out=s[:, 3:4]), deps=[t3_, tk_o])
            t5_ = P.act(lambda e: e.activation(out=s[:, 4:5], in_=s[:, 3:4], func=AF.Ln, scale=1.0, bias=c.eps_t[:, 0:1]),
                        deps=[t4_])
            t6_ = P.act(lambda e: e.activation(out=s[:, 5:6], in_=s[:, 4:5], func=AF.Exp, scale=-0.5), deps=[t5_])
            t7_ = P.dve(lambda e: e.scalar_tensor_tensor(out=s[:, 6:7], in0=s[:, 2:3], scalar=16.0, in1=s[:, 5:6],
                                                         op0=ALU.mult, op1=ALU.mult), deps=[t6_, t3_])
            t_y = P.dve(lambda e: e.scalar_tensor_tensor(out=yb[p][:], in0=psO[:, 0:256], scalar=s[:, 6:7], in1=og[p][:],
                                                         op0=ALU.mult, op1=ALU.mult), deps=[t7_, a["t_og"], tk_o])
            fO.add(t_y)
            f_og[p].add(t_y)
            a["t_y"] = t_y

        def C(h, tt, ys, last_chunk_tokens):
            p = tt % 2
            a = A[tt]
            P.pe(_mm(psY[:, 0:128], yb[p][:, 0:128], c.identb[:], True, True), deps=[a["t_y"]] + fY.take())
            tk_y = P.pe(_mm(psY[:, 128:256], yb[p][:, 128:256], c.identb[:], True, True), sig=True)
            f_yb[p].add(tk_y)
            t_ys = P.act(lambda e: e.activation(out=ys[:, :, tt * 128:(tt + 1) * 128],
                                                in_=psY.rearrange("p (j s) -> p j s", j=2), func=AF.Copy),
                         deps=[tk_y] + (f_ys[h % 2].take() if tt == 0 else []))
            fY.add(t_ys)
            last_chunk_tokens.append(t_ys)

        for h in range(8):
            if h + 1 < 8:
                wm.load(h + 1, "pool", T["m_wh"][h + 1])
            W = wm.tile(h)
            wtok = wm.rdy(h)
            ys = ystage[h % 2]
            t_z1 = P.pool(lambda e: e.memset(CT32[:], 0.0), deps=f_ct32.take())
            t_z2 = P.pool(lambda e: e.memset(CTb[0][:], 0.0), deps=f_CTb[0].take() + f_CTb[1].take())
            f_ct32.add(t_z1)
            state["cur"] = 0
            state["t_ctb"] = t_z2
            lct = []
            for step in range(18):
                if step < 16:
                    A1(h, step, W, wtok)
                if 1 <= step <= 16:
                    B1(h, step - 1)
                if step < 16:
                    A2(h, step, W, wtok)
                if 1 <= step <= 16:
                    B2(h, step - 1)
                if step >= 2:
                    C(h, step - 2, ys, lct)
                if step < 16 and (h * 16 + step) % 3 == 2:
                    bg_step()
            wm.release(h, lct[-1])
            tok = P.dma("sp", T["yTd"][:, 2 * h:2 * h + 2, :], ys[:], f"ST_ys{h % 2}", deps=[lct[-1]])
            f_ys[h % 2].add(tok)
            c.last["yTd"] = c.last.get("yTd", []) + [tok]
        while not bg_state["done"]:
            bg_step()


def make_norm_epi(c, sb, bank, sq=None, rstd_slots=None, off_dve=False):
    P = c.P
    if sq is None:
        sq = Slots([sb("e_sq0", [128, 512], F32), sb("e_sq1", [128, 512], F32)])
    if rstd_slots is None:
        rstd_slots = Slots([sb("e_rstd", [128, 512], F32)])
    st_ = {"bank_free": []}

    def epi_steps(xnt, tq, li, deps_x, deps_mod, deps_h, out_dram=None, res=None):
        if li < 4:
            l, s_ = (li, 0) if li < 2 else (li - 2, 1)
            gvec = c.gm[:, l * 2 + s_, :]
            shv = mod_col(c, l, 0 if s_ == 0 else 3)
        else:
            gvec = c.nrm[:, 4, :]
            shv = None
        tk = None
        rstd, rdeps, rslot = rstd_slots.next()
        for fc in range(16):
            sqt, sdeps, sslot = sq.next()
            t_sq = P.act(lambda e, sqt=sqt, fc=fc: e.activation(out=sqt[:], in_=xnt[:, fc, :], func=AF.Square),
                         deps=list(deps_x) + sdeps)
            tk = P.pe(_mm(bank[:], c.ones_m[:], sqt[:], fc == 0, fc == 15),
                      deps=[t_sq] + (st_["bank_free"] if fc == 0 else []), sig=True)
            sq.release(sslot, tk)
            yield
        t_r0 = P.act(lambda e: e.activation(out=rstd[:], in_=bank[:], func=AF.Sqrt, bias=c.eps_t[:, 0:1], scale=1.0),
                     deps=[tk] + rdeps)
        st_["bank_free"] = [t_r0]
        t_r = P.dve(lambda e: e.reciprocal(out=rstd[:], in_=rstd[:]), deps=[t_r0])
        last = None
        for fc in range(16):
            if off_dve:
                t1 = P.pool(lambda e, fc=fc: e.tensor_tensor(out=xnt[:, fc, :], in0=xnt[:, fc, :], in1=rstd[:], op=ALU.mult),
                            deps=[t_r] + list(deps_mod))
                if li < 4:
                    last = P.act(lambda e, fc=fc: e.activation(out=c.hT[:, fc, tq * 512:(tq + 1) * 512], in_=xnt[:, fc, :],
                                                               func=AF.Identity, scale=gvec[:, fc:fc + 1], bias=shv[:, fc:fc + 1]),
                                 deps=[t1] + (list(deps_h) if fc == 0 else []))
                else:
                    last = P.act(lambda e, fc=fc: e.activation(out=xnt[:, fc, :], in_=xnt[:, fc, :], func=AF.Copy,
                                                               scale=gvec[:, fc:fc + 1]), deps=[t1])
                yield
                continue
            t1 = P.dve(lambda e, fc=fc: e.scalar_tensor_tensor(out=xnt[:, fc, :], in0=xnt[:, fc, :], scalar=gvec[:, fc:fc + 1],
                                                               in1=rstd[:], op0=ALU.mult, op1=ALU.mult),
                       deps=[t_r] + list(deps_mod))
            last = t1
            if li < 4:
                last = P.dve(lambda e, fc=fc: e.tensor_scalar(out=c.hT[:, fc, tq * 512:(tq + 1) * 512], in0=xnt[:, fc, :],
                                                              scalar1=shv[:, fc:fc + 1], scalar2=None, op0=ALU.add),
                             deps=[t1] + (list(deps_h) if fc == 0 else []))
            yield
        rstd_slots.release(rslot, last)
        if li == 4:
            tok = P.dma("sp", out_dram[:, :, tq * 512:(tq + 1) * 512], xnt[:], "ST_final", deps=[last])
            c.final.append(tok)
            last = tok
        c.last["hT_tiles"] = c.last.get("hT_tiles", []) + [last]
        if res is not None:
            res.append(last)

    def epi(xnt, tq, li, deps_x, deps_mod, deps_h, out_dram=None):
        res = []
        for _ in epi_steps(xnt, tq, li, deps_x, deps_mod, deps_h, out_dram=out_dram, res=res):
            pass
        return res[0]
    epi.steps = epi_steps
    return epi


def phase_outproj(c, w_dram, nk, gcol, x_src, x_dst, next_li):
    import contextlib
    nc, P, T = c.nc, c.P, c.T
    B = c.banks
    with contextlib.ExitStack() as st:
        _u = _uid()

        def sb(name, shape, dt_):
            return st.enter_context(nc.sbuf_tensor(_u + "o_" + name, shape, dt_))
        W = sb("w", [128, 16, nk, 128], BF16)
        t_w = [P.dma("pool", W[:, fo], w_dram[fo], f"LD_ow{fo % 4}") for fo in range(16)]
        t_src = [P.dma("sp", c.hT[:, :, tq * 512:(tq + 1) * 512], T["yTd"][:, :, tq * 512:(tq + 1) * 512], f"LD_osrc{tq}",
                       deps=c.last.get("yTd", [])) for tq in range(4)]
        xr = Ring(P, "ox", [sb(f"x{i}", [128, 512], F32) for i in range(2)])
        xnts = [sb(f"xnt{i}", [128, 16, 512], F32) for i in range(2)]
        epi = make_norm_epi(c, sb, B[6])
        ps = Slots([B[0], B[1], B[2], B[3]])
        seq = [(tq, fo) for tq in range(4) for fo in range(16)]
        for j in range(1):
            xr.load(j, "sp", x_src[:, seq[j][1], seq[j][0] * 512:(seq[j][0] + 1) * 512])
        xnt_free = [[], []]
        pend = None
        pend_res = []
        for tq in range(4):
            xnt = xnts[tq % 2]
            stores = []
            for fo in range(16):
                m = tq * 16 + fo
                if m + 1 < len(seq):
                    tq2, fo2 = seq[m + 1]
                    xr.load(m + 1, "sp", x_src[:, fo2, tq2 * 512:(tq2 + 1) * 512])
                pt, pdeps, pslot = ps.next()
                for kc in range(nk):
                    tk = P.pe(_mm(pt[:], W[:, fo, kc, :], c.hT[:, kc, tq * 512:(tq + 1) * 512], kc == 0, kc == nk - 1),
                              deps=[t_w[fo], t_src[tq]] + pdeps if kc == 0 else (), sig=(kc == nk - 1))
                xt = xr.tile(m)
                t_e = P.dve(lambda e, pt=pt, xt=xt, fo=fo, xnt=xnt: e.scalar_tensor_tensor(
                    out=xnt[:, fo, :], in0=pt[:], scalar=gcol[:, fo:fo + 1], in1=xt[:], op0=ALU.mult, op1=ALU.add),
                    deps=[tk, xr.rdy(m)] + (xnt_free[tq % 2] if fo == 0 else []))
                ps.release(pslot, t_e)
                xr.release(m, t_e)
                tok = P.dma("sp", x_dst[:, fo, tq * 512:(tq + 1) * 512], xnt[:, fo, :], f"ST_oxn{fo % 4}", deps=[t_e])
                stores.append(tok)
                c.last["x"] = c.last.get("x", []) + [tok]
                if fo >= 2 and pend is not None:
                    for _ in range(3):
                        try:
                            next(pend)
                        except StopIteration:
                            xnt_free[(tq - 1) % 2] = list(pend_res)
                            pend = None
                            break
            if pend is not None:
                for _ in pend:
                    pass
                xnt_free[(tq - 1) % 2] = list(pend_res)
                pend = None
            if tq == 3:
                xnt_free[tq % 2] = [epi(xnt, tq, next_li, [t_e], stores, [tk])]
            else:
                pend_res = []
                pend = epi.steps(xnt, tq, next_li, [t_e], list(stores), [tk], res=pend_res)


def phase_ffn(c, l, x_src, x_dst, next_li, out_dram=None):
    import contextlib
    nc, P, T = c.nc, c.P, c.T
    B = c.banks
    gcol = mod_col(c, l, 5)
    with contextlib.ExitStack() as st:
        _u = _uid()

        def sb(name, shape, dt_):
            return st.enter_context(nc.sbuf_tensor(_u + "ff_" + name, shape, dt_))
        wu = Ring(P, "fwu", [sb(f"wu{i}", [128, 16, 256], BF16) for i in range(3)])
        wd = Ring(P, "fwd", [sb(f"wd{i}", [128, 22, 128], BF16) for i in range(3)])
        act = sb("act", [128, 44, 512], BF16)
        cw = sb("cw", [128, 3, 88], F32)
        cb = sb("cb", [128, 88], F32)
        halo = [sb(f"halo{i}", [128, 88, 2], F32) for i in range(2)]
        tg = Slots([sb(f"tg{i}", [128, 512], F32) for i in range(2)])
        tv = Slots([sb(f"tv{i}", [128, 512], F32) for i in range(2)])
        sg = Slots([sb(f"sg{i}", [128, 512], F32) for i in range(2)])
        xr = Ring(P, "fx", [sb(f"x{i}", [128, 512], F32) for i in range(2)])
        e_rstd = Slots([sb("e_rstd", [128, 512], F32)])
        xnt = sb("xnt", [128, 16, 512], F32)
        epi = make_norm_epi(c, sb, B[6], sq=sg, rstd_slots=e_rstd, off_dve=True)
        xnt_free = []
        pend_epi = None
        pend_res = []
        t_cw = P.dma("sp", cw[:], T["f_cw"][:, l], "LD_fcw")
        t_cb = P.dma("sp", cb[:], T["f_cb"][:, l], "LD_fcb")
        psG = Slots([B[0], B[1]])
        psV = Slots([B[2], B[3]])
        psD = Slots([B[4], B[5]])
        act_free = []
        halo_w = [[None] * 88, [None] * 88]
        halo_r = [[None] * 88, [None] * 88]
        NP_ = 44
        nload = 0
        seq = [(tq, i) for tq in range(4) for i in range(NP_)]
        for j in range(3):
            wu.load(j, "pool", T["f_wu"][l, seq[j][1]])
        wd_seq = [(tq, fo) for tq in range(4) for fo in range(16)]
        def wd_src(m2):
            fo_ = wd_seq[m2 // 2][1]
            hf = m2 % 2
            return T["f_wd"][l, fo_][:, hf * 22:(hf + 1) * 22, :]
        for j in range(3):
            wd.load(j, "pool", wd_src(j))
        xr_i = 0
        xr.load(0, "sp", x_src[:, 0, 0:512])
        n_x = 2
        for tq in range(4):
            hp, hw = (tq - 1) % 2, tq % 2
            t_act = []
            for i in range(NP_):
                n = tq * NP_ + i
                W = wu.tile(n)
                pg, gdeps, gslot = psG.next()
                pv, vdeps, vslot = psV.next()
                for kc in range(16):
                    rhs = c.hT[:, kc, tq * 512:(tq + 1) * 512]
                    P.pe(_mm(pg[:], W[:, kc, 0:128], rhs, kc == 0, kc == 15),
                         deps=[wu.rdy(n)] + gdeps + vdeps if kc == 0 else ())
                    tk = P.pe(_mm(pv[:], W[:, kc, 128:256], rhs, kc == 0, kc == 15), sig=(kc == 15))
                wu.release(n, tk)
                t_up_last = tk
                if i >= 3 and pend_epi is not None:
                    try:
                        next(pend_epi)
                    except StopIteration:
                        xnt_free = list(pend_res)
                        pend_epi = None
                if n + 3 < len(seq):
                    wu.load(n + 3, "pool", T["f_wu"][l, seq[n + 3][1]])
                outs = []
                for (pp, ch, slots_, pslots, pslot) in ((pg, i, tg, psG, gslot), (pv, 44 + i, tv, psV, vslot)):
                    t_, tdeps, tslot = slots_.next()
                    t_e = P.act(lambda e, t_=t_, pp=pp, ch=ch: e.activation(out=t_[:], in_=pp[:], func=AF.Identity,
                                                                           scale=cw[:, 2, ch:ch + 1], bias=cb[:, ch:ch + 1]),
                                deps=[tk, t_cw, t_cb] + tdeps)
                    t_1 = P.dve(lambda e, t_=t_, pp=pp, ch=ch: e.scalar_tensor_tensor(
                        out=t_[:, 1:512], in0=pp[:, 0:511], scalar=cw[:, 1, ch:ch + 1], in1=t_[:, 1:512],
                        op0=ALU.mult, op1=ALU.add), deps=[t_e])
                    t_2 = P.dve(lambda e, t_=t_, pp=pp, ch=ch: e.scalar_tensor_tensor(
                        out=t_[:, 2:512], in0=pp[:, 0:510], scalar=cw[:, 0, ch:ch + 1], in1=t_[:, 2:512],
                        op0=ALU.mult, op1=ALU.add), deps=[t_1])
                    t_h = P.dve(lambda e, pp=pp, ch=ch, hw=hw: e.tensor_copy(out=halo[hw][:, ch, :], in_=pp[:, 510:512]),
                                deps=[t_2, halo_r[hw][ch]])
                    halo_w[hw][ch] = t_h
                    pslots.release(pslot, t_h)
                    last = t_2
                    if tq > 0:
                        t_3 = P.dve(lambda e, t_=t_, ch=ch, hp=hp: e.scalar_tensor_tensor(
                            out=t_[:, 0:1], in0=halo[hp][:, ch, 1:2], scalar=cw[:, 1, ch:ch + 1], in1=t_[:, 0:1],
                            op0=ALU.mult, op1=ALU.add), deps=[t_2, halo_w[hp][ch]])
                        t_4 = P.dve(lambda e, t_=t_, ch=ch, hp=hp: e.scalar_tensor_tensor(
                            out=t_[:, 0:2], in0=halo[hp][:, ch, 0:2], scalar=cw[:, 0, ch:ch + 1], in1=t_[:, 0:2],
                            op0=ALU.mult, op1=ALU.add), deps=[t_3])
                        halo_r[hp][ch] = t_4
                        last = t_4
                    outs.append((t_, last, slots_, tslot))
                (tgt, tg_last, _, tg_slot), (tvt, tv_last, _, tv_slot) = outs
                sgt, sdeps, sslot = sg.next()
                t_s = P.act(lambda e, sgt=sgt, tgt=tgt: e.activation(out=sgt[:], in_=tgt[:], func=AF.Silu),
                            deps=[tg_last] + sdeps)
                tg.release(tg_slot, t_s)
                t_m = P.pool(lambda e, sgt=sgt, tvt=tvt, i=i: e.tensor_tensor(out=act[:, i, :], in0=sgt[:], in1=tvt[:], op=ALU.mult),
                             deps=[t_s, tv_last] + (act_free if i == 0 else []))
                sg.release(sslot, t_m)
                tv.release(tv_slot, t_m)
                t_act.append(t_m)
            if pend_epi is not None:
                for _ in pend_epi:
                    pass
                xnt_free = list(pend_res)
                pend_epi = None
            act_free = []
            stores = []
            for fo in range(16):
                m = tq * 16 + fo
                pd, ddeps, dslot = psD.next()
                for hf in range(2):
                    m2 = 2 * m + hf
                    Wd = wd.tile(m2)
                    for c2 in range(22):
                        cc = hf * 22 + c2
                        tk = P.pe(_mm(pd[:], Wd[:, c2, :], act[:, cc, :], cc == 0, cc == NP_ - 1),
                                  deps=[t_act[cc]] + (([wd.rdy(m2)] + (ddeps if cc == 0 else [])) if c2 == 0 else []),
                                  sig=(c2 == 21))
                    wd.release(m2, tk)
                    if m2 + 3 < 2 * len(wd_seq):
                        wd.load(m2 + 3, "pool", wd_src(m2 + 3))
                xt = xr.tile(m)
                t_e = P.dve(lambda e, pd=pd, fo=fo, xt=xt: e.scalar_tensor_tensor(
                    out=xnt[:, fo, :], in0=pd[:], scalar=gcol[:, fo:fo + 1], in1=xt[:], op0=ALU.mult, op1=ALU.add),
                    deps=[tk, xr.rdy(m)] + (xnt_free if fo == 0 else []))
                psD.release(dslot, t_e)
                xr.release(m, t_e)
                if m + 1 < len(wd_seq):
                    tq2, fo2 = wd_seq[m + 1]
                    xr.load(m + 1, "sp", x_src[:, fo2, tq2 * 512:(tq2 + 1) * 512])
                if x_dst is not None:
                    tok = P.dma("sp", x_dst[:, fo, tq * 512:(tq + 1) * 512], xnt[:, fo, :], f"ST_fxo{fo % 4}", deps=[t_e])
                    stores.append(tok)
                    c.last["x"] = c.last.get("x", []) + [tok]
                if fo == 15:
                    act_free = [tk]
            if tq == 3:
                xnt_free = [epi(xnt, tq, next_li, [t_e], stores, [t_up_last], out_dram=out_dram)]
            else:
                pend_res = []
                pend_epi = epi.steps(xnt, tq, next_li, [t_e], list(stores), [t_up_last], out_dram=out_dram, res=pend_res)


ROPE_THETA = 500000.0


def phase_attn(c):
    import contextlib
    nc, P, T = c.nc, c.P, c.T
    B = c.banks

    class Fr:
        def __init__(self, init=()):
            self.t = list(init)

        def take(self):
            t = self.t
            self.t = []
            return t

        def add(self, *toks):
            self.t += [x for x in toks if x is not None]

    with contextlib.ExitStack() as st:
        _u = _uid()

        def sb(name, shape, dt_):
            return st.enter_context(nc.sbuf_tensor(_u + "at_" + name, shape, dt_))
        posi = sb("posi", [128, 48], I32)
        posf = sb("posf", [128, 48], F32)
        ang = sb("ang", [128, 48, 16], F32)
        angs = sb("angs", [128, 48, 16], F32)
        angc = sb("angc", [128, 48, 16], F32)
        sint = sb("sin", [128, 48, 16], F32)
        cost = sb("cos", [128, 48, 16], F32)
        mpi = sb("mpi", [128, 1], F32)
        maskT = sb("maskT", [128, 2, 128], F32)
        t_p = P.dma("sp", posi[:], T["posu"].rearrange("p g u -> p (g u)"), "LD_pos")
        t_mpi = P.pool(lambda e: e.memset(mpi[:], -math.pi))
        t_m1 = P.pool(lambda e: e.tensor_copy(out=maskT[:, 0, :], in_=c.tri[:]), deps=[c.tok_consts])
        t_m2 = P.pool(lambda e: e.tensor_copy(out=maskT[:, 1, :], in_=c.tri_ge[:]), deps=[c.tok_consts])
        t_pf = P.dve(lambda e: e.tensor_copy(out=posf[:], in_=posi[:]), deps=[t_p])
        t_a = None
        for j in range(16):
            invf = float(np.float32(ROPE_THETA) ** np.float32(-j / 16.0))
            t_a = P.dve(lambda e, j=j, invf=invf: e.tensor_scalar(out=ang[:, :, j], in0=posf[:], scalar1=invf, scalar2=None,
                                                                  op0=ALU.mult), deps=[t_pf])
        MAGIC = 12582912.0
        inv2pi = 1.0 / (2.0 * math.pi)
        qq = sb("qq", [128, 48, 16], F32)
        t_q = P.dve(lambda e: e.tensor_scalar(out=qq[:], in0=ang[:], scalar1=inv2pi, scalar2=None, op0=ALU.mult), deps=[t_a])
        outs_ = []
        for (dst, off, wk) in ((sint, 0.0, angs), (cost, 0.25, angc)):
            t0 = P.dve(lambda e, wk=wk, off=off: e.tensor_scalar(out=wk[:], in0=qq[:], scalar1=off, scalar2=None, op0=ALU.add), deps=[t_q])
            t1 = P.dve(lambda e, dst=dst, wk=wk: e.tensor_scalar(out=dst[:], in0=wk[:], scalar1=MAGIC, scalar2=None, op0=ALU.add), deps=[t0])
            t2 = P.dve(lambda e, dst=dst: e.tensor_scalar(out=dst[:], in0=dst[:], scalar1=-MAGIC, scalar2=None, op0=ALU.add), deps=[t1])
            t3 = P.dve(lambda e, dst=dst, wk=wk: e.tensor_tensor(out=wk[:], in0=wk[:], in1=dst[:], op=ALU.subtract), deps=[t2])
            t4 = P.act(lambda e, dst=dst, wk=wk: e.activation(out=dst[:], in_=wk[:], func=AF.Sin, scale=2.0 * math.pi), deps=[t3])
            outs_.append(t4)
        t_sin, t_cos = outs_
        t_rope = [t_sin, t_cos]
        if "rope" in c.dbg:
            c.bar = t_rope
            dbg_dump(c, "sin", sint, [128, 48, 16], F32)
            dbg_dump(c, "cos", cost, [128, 48, 16], F32)

        wa = Ring(P, "awi", [sb(f"wa{i}", [128, 16, 384], BF16) for i in range(2)])
        R = [sb(f"R{i}", [128, 2, 16, 32], F32) for i in range(2)]
        qkb = [sb(f"qkb{i}", [128, 2, 16, 128], BF16) for i in range(2)]
        Vb = [sb(f"Vb{i}", [128, 16, 128], BF16) for i in range(2)]
        qkT = sb("qkT", [128, 2, 16, 128], BF16)
        tA = sb("tA", [128, 16, 16], F32)
        tB = sb("tB", [128, 16, 16], F32)
        E = [sb(f"E{i}", [128, 2, 128], BF16) for i in range(4)]
        PT = [sb(f"PT{i}", [128, 2, 128], BF16) for i in range(4)]
        accN = sb("accN", [128, S], F32)
        accD = sb("accD", [128, S], F32)
        ost = [sb(f"ost{i}", [128, S], BF16) for i in range(2)]
        f_R, f_qkb, f_Vb = [Fr(), Fr()], [Fr(), Fr()], [Fr(), Fr()]
        f_qkT, f_tA, f_E, f_PT = Fr(), Fr(), [Fr() for _ in range(4)], [Fr() for _ in range(4)]
        f_acc = Fr()
        f_ost = [Fr(), Fr()]
        psP = [B[0], B[1]]
        f_psP = [Fr(), Fr()]
        psT = [B[2], B[3]]
        f_psT = [Fr(), Fr()]
        psS = [B[4], B[5]]
        f_psS = [Fr(), Fr()]
        psND = [B[6], B[7]]
        f_psND = [Fr(), Fr()]
        cnt = {"P": 0, "T": 0, "S": 0, "ND": 0, "E": 0}
        seq = [(h, g) for h in range(16) for g in range(3)]
        for j in range(2):
            wa.load(j, "pool", T["a_wi"][seq[j][0], seq[j][1]])
        Atok = {}

        def utok(g, u):
            d = ATT_GROUPS[g][1]
            nb = S // d // 128
            r, n = u // nb, u % nb
            start = r + d * 128 * n
            return slice(start, start + d * 127 + 1, d) if d > 1 else slice(start, start + 128)

        def projA(idx):
            h, g = seq[idx]
            b = idx % 2
            W = wa.tile(idx)
            evs = []
            for u in range(16):
                if u > 0:
                    yield
                s_ = cnt["P"] % 2
                cnt["P"] += 1
                pp = psP[s_]
                ts = utok(g, u)
                for kc in range(16):
                    tk = P.pe(_mm(pp[:, 0:384], c.hT[:, kc, ts], W[:, kc, :], kc == 0, kc == 15),
                              deps=[wa.rdy(idx)] + f_psP[s_].take() if kc == 0 else (), sig=(kc == 15))
                ppv = pp[:, 0:256].rearrange("p (a d) -> p a d", a=2)
                t1 = P.act(lambda e, ppv=ppv, u=u: e.activation(out=R[b][:, :, u, :], in_=ppv[:, :, 0:32], func=AF.Copy),
                           deps=[tk] + (f_R[b].take() if u == 0 else []))
                t2 = P.act(lambda e, ppv=ppv, u=u: e.activation(out=qkb[b][:, :, u, 32:128], in_=ppv[:, :, 32:128], func=AF.Copy),
                           deps=[tk] + (f_qkb[b].take() if u == 0 else []))
                t3 = P.dve(lambda e, pp=pp, u=u: e.tensor_copy(out=Vb[b][:, u, :], in_=pp[:, 256:384]),
                           deps=[tk, t1, t2] + (f_Vb[b].take() if u == 0 else []))
                f_psP[s_].add(t3)
                evs += [t1, t2, t3]
            wa.release(idx, tk)
            if idx + 2 < len(seq):
                wa.load(idx + 2, "pool", T["a_wi"][seq[idx + 2][0], seq[idx + 2][1]])
            Atok[idx] = evs[-3:]
            yield

        def pull(gen, n):
            if gen is None:
                return
            for _ in range(n):
                try:
                    next(gen)
                except StopIteration:
                    return

        Rtok = {}

        def ropeB(idx):
            h, g = seq[idx]
            b = idx % 2
            a = Atok[idx]
            last_rope = []
            for qk in range(2):
                t1v = R[b][:, qk, :, 0:16]
                t2v = R[b][:, qk, :, 16:32]
                cg = cost[:, g * 16:(g + 1) * 16, :]
                sg_ = sint[:, g * 16:(g + 1) * 16, :]
                ta = P.pool(lambda e, t1v=t1v, cg=cg: e.tensor_tensor(out=tA[:], in0=t1v, in1=cg, op=ALU.mult),
                            deps=a + t_rope + f_tA.take())
                tb = P.pool(lambda e, t2v=t2v, sg_=sg_: e.tensor_tensor(out=tB[:], in0=t2v, in1=sg_, op=ALU.mult), deps=[ta])
                to1 = P.pool(lambda e, qk=qk: e.tensor_tensor(out=qkb[b][:, qk, :, 0:16], in0=tA[:], in1=tB[:], op=ALU.subtract),
                             deps=[ta, tb])
                tc_ = P.pool(lambda e, t2v=t2v, cg=cg: e.tensor_tensor(out=tA[:], in0=t2v, in1=cg, op=ALU.mult), deps=[to1])
                td = P.pool(lambda e, t1v=t1v, sg_=sg_: e.tensor_tensor(out=tB[:], in0=t1v, in1=sg_, op=ALU.mult), deps=[to1])
                to2 = P.pool(lambda e, qk=qk: e.tensor_tensor(out=qkb[b][:, qk, :, 16:32], in0=tA[:], in1=tB[:], op=ALU.add),
                             deps=[tc_, td])
                f_tA.add(to2)
                last_rope = last_rope + [to1, to2]
            f_R[b].add(*last_rope)
            Rtok[idx] = last_rope

        def attnB(idx, gen):
            h, g = seq[idx]
            b = idx % 2
            d = ATT_GROUPS[g][1]
            nb = S // d // 128
            a = Atok[idx]
            last_rope = Rtok[idx]
            t_tr = []
            qkT_free = f_qkT.take()
            for qk in range(2):
                for u4 in range(4):
                    s_ = cnt["T"] % 2
                    cnt["T"] += 1
                    pt = psT[s_]
                    for uu in range(4):
                        u = u4 * 4 + uu
                        tk = P.pe(_mm(pt[:, uu * 128:(uu + 1) * 128], qkb[b][:, qk, u, :], c.identb[:], True, True),
                                  deps=last_rope + a + f_psT[s_].take() if uu == 0 else (), sig=(uu == 3))
                    if s_ == 0:
                        te = P.dve(lambda e, pt=pt, qk=qk, u4=u4: e.tensor_copy(
                            out=qkT[:, qk, u4 * 4:(u4 + 1) * 4, :], in_=pt[:, :].rearrange("p (u s) -> p u s", u=4)),
                            deps=[tk] + qkT_free)
                    else:
                        te = P.act(lambda e, pt=pt, qk=qk, u4=u4: e.activation(
                            out=qkT[:, qk, u4 * 4:(u4 + 1) * 4, :], in_=pt[:, :].rearrange("p (u s) -> p u s", u=4), func=AF.Copy),
                            deps=[tk] + qkT_free)
                    f_psT[s_].add(te)
                    t_tr.append(te)
                    if u4 % 2 == 1:
                        pull(gen, 1)
            f_qkb[b].add(tk)
            def s_stage(pj):
                pts = []
                for uu in range(2):
                    u = pj * 2 + uu
                    n = u % nb
                    blks = [u] + ([u - 1] if n > 0 else [])
                    s_s = cnt["S"] % 2
                    cnt["S"] += 1
                    ps_ = psS[s_s]
                    for bi, ub in enumerate(blks):
                        tk = P.pe(_mm(ps_[:, bi * 128:(bi + 1) * 128], qkT[:, 1, ub, :], qkT[:, 0, u, :], True, True),
                                  deps=t_tr + f_psS[s_s].take() if bi == 0 else (), sig=(bi == len(blks) - 1))
                    nbk = len(blks)
                    eb = cnt["E"] % 4
                    cnt["E"] += 1
                    te = P.act(lambda e, ps_=ps_, eb=eb, nbk=nbk: e.activation(
                        out=E[eb][:, 0:nbk, :], in_=ps_[:, 0:nbk * 128].rearrange("p (a s) -> p a s", a=nbk), func=AF.Exp,
                        scale=128.0 ** -0.5), deps=[tk] + f_E[eb].take())
                    f_psS[s_s].add(te)
                    tm = P.dve(lambda e, eb=eb, nbk=nbk: e.tensor_tensor(out=PT[eb][:, 0:nbk, :], in0=E[eb][:, 0:nbk, :],
                                                                          in1=maskT[:, 0:nbk, :], op=ALU.mult),
                                deps=[te, t_m1, t_m2] + f_PT[eb].take())
                    f_E[eb].add(tm)
                    pts.append((eb, blks, tm))
                return pts

            def av_stage(pj, pts):
                s_nd = cnt["ND"] % 2
                cnt["ND"] += 1
                pnd = psND[s_nd]
                first_nd = True
                tk_nd = None
                for uu, (eb, blks, tm) in enumerate(pts):
                    for bi, ub in enumerate(blks):
                        P.pe(_mm(pnd[:, uu * 128:(uu + 1) * 128], Vb[b][:, ub, :], PT[eb][:, bi, :], bi == 0, bi == len(blks) - 1),
                             deps=[tm, a[2]] + (f_psND[s_nd].take() if first_nd else []))
                        first_nd = False
                    for bi, ub in enumerate(blks):
                        tk_nd = P.pe(_mm(pnd[:, 256 + uu * 128:256 + (uu + 1) * 128], c.onesb[:], PT[eb][:, bi, :],
                                         bi == 0, bi == len(blks) - 1), sig=(bi == len(blks) - 1))
                    f_PT[eb].add(tk_nd)
                if g == 0:
                    vN, vD = accN[:, pj * 256:(pj + 1) * 256], accD[:, pj * 256:(pj + 1) * 256]
                    pN, pD = pnd[:, 0:256], pnd[:, 256:512]
                elif g == 1:
                    r, n0 = pj // 2, (pj % 2) * 2
                    st_ = r + 512 * n0
                    vN, vD = accN[:, st_:st_ + 1021:4], accD[:, st_:st_ + 1021:4]
                    pN, pD = pnd[:, 0:256], pnd[:, 256:512]
                else:
                    r0 = 2 * pj
                    vN = accN[:].rearrange("p (i r) -> p r i", r=16)[:, r0:r0 + 2, :]
                    vD = accD[:].rearrange("p (i r) -> p r i", r=16)[:, r0:r0 + 2, :]
                    pN = pnd[:, 0:256].rearrange("p (a i) -> p a i", a=2)
                    pD = pnd[:, 256:512].rearrange("p (a i) -> p a i", a=2)
                if g == 0:
                    t1 = P.dve(lambda e, vN=vN, pN=pN: e.tensor_copy(out=vN, in_=pN), deps=[tk_nd] + (f_acc.take() if pj == 0 else []))
                    t2 = P.dve(lambda e, vD=vD, pD=pD: e.tensor_copy(out=vD, in_=pD), deps=[tk_nd])
                else:
                    t1 = P.dve(lambda e, vN=vN, pN=pN: e.tensor_tensor(out=vN, in0=pN, in1=vN, op=ALU.add), deps=[tk_nd, Atok["acc"]])
                    t2 = P.dve(lambda e, vD=vD, pD=pD: e.tensor_tensor(out=vD, in0=pD, in1=vD, op=ALU.add), deps=[tk_nd, Atok["acc"]])
                f_psND[s_nd].add(t1, t2)
                Atok["acc_last"] = t2
                return tk_nd

            nxt_pts = s_stage(0)
            for pj in range(8):
                cur_pts = nxt_pts
                if pj + 1 < 8:
                    nxt_pts = s_stage(pj + 1)
                pull(gen, 1)
                tk_nd = av_stage(pj, cur_pts)
            Atok["acc"] = Atok["acc_last"]
            f_Vb[b].add(tk_nd)
            f_qkT.add(tk_nd)
            if g == 2:
                ob = h % 2
                tr = P.dve(lambda e: e.reciprocal(out=accD[:], in_=accD[:]), deps=[Atok["acc"]])
                to = P.dve(lambda e, ob=ob: e.tensor_tensor(out=ost[ob][:], in0=accN[:], in1=accD[:], op=ALU.mult),
                           deps=[tr] + f_ost[ob].take())
                f_acc.add(to)
                tok = P.dma("sp", T["yTd"][:, h, :], ost[ob][:], f"ST_ao{ob}", deps=[to] + (c.last.get("src_loaded", []) if h < 2 else []))
                f_ost[ob].add(tok)
                c.last["yTd"] = c.last.get("yTd", []) + [tok]

        c.last["yTd"] = []
        for idx in range(len(seq) + 1):
            gen = projA(idx) if idx < len(seq) else None
            if idx >= 1:
                ropeB(idx - 1)
            pull(gen, 4)
            if idx >= 1:
                attnB(idx - 1, gen)
            pull(gen, 20)


_PROG_CACHE = {}


def kernel(**inputs):
    inp = {k: np.asarray(v) for k, v in inputs.items()}
    if "nc" not in _PROG_CACHE:
        _PROG_CACHE["nc"] = build_program()
    nc = _PROG_CACHE["nc"]
    sh = pack_shared(inp)
    in_maps = [dict(sh, **pack_core(inp, b)) for b in range(N_CORES)]
    res = run_bass_kernel_spmd(nc, in_maps, core_ids=list(range(N_CORES)))
    out = np.empty((N_CORES, S, D), dtype=np.float32)
    for b in range(N_CORES):
        oT = np.asarray(res.results[b]["outT"])
        out[b] = oT.transpose(2, 1, 0).reshape(S, D)
    return out
```

```python
import math
import numpy as np
import concourse.bass as bass
import concourse.mybir as mybir
from concourse.bass_utils import run_bass_kernel_spmd

F32 = mybir.dt.float32
BF16 = mybir.dt.bfloat16
I32 = mybir.dt.int32
AF = mybir.ActivationFunctionType
ALU = mybir.AluOpType
AX = mybir.AxisListType

S = 2048
D = 2048
NCH = 16
DFF = 5632
NFF = 44
EPS = 1e-6
N_CORES = 8


class Prog:
    ENG = ("pe", "act", "dve", "pool", "sp")

    def __init__(self, nc):
        self.nc = nc
        self.q = {e: [] for e in self.ENG}
        self.cnt = {}
        self.waited = {e: {} for e in self.ENG}
        self.sem_keys = []
        for e in self.ENG:
            self._key("E_" + e)

    def _key(self, k):
        if k not in self.cnt:
            self.cnt[k] = 0
            self.sem_keys.append(k)
        return k

    def _waits(self, eng, deps):
        ws = []
        for d in deps:
            if d is None:
                continue
            if isinstance(d, list):
                ws += self._waits(eng, d)
                continue
            k, v = d
            if self.waited[eng].get(k, 0) < v:
                self.waited[eng][k] = v
                ws.append((k, v))
        return ws

    def op(self, eng, fn, deps=(), sig=True):
        ws = self._waits(eng, deps)
        tok = None
        if sig:
            k = "E_" + eng
            self.cnt[k] += 1
            tok = (k, self.cnt[k])
        self.q[eng].append((ws, fn, ("E_" + eng, 1) if sig else None))
        return tok

    def dma(self, eng, out, in_, semkey, deps=()):
        self._key(semkey)
        ws = self._waits(eng, deps)
        self.cnt[semkey] += 16
        tok = (semkey, self.cnt[semkey])
        self.q[eng].append((ws, lambda e, o=out, i=in_: e.dma_start(out=o, in_=i), (semkey, 16)))
        return tok

    def pe(self, fn, deps=(), sig=False):
        return self.op("pe", fn, deps, sig)

    def act(self, fn, deps=()):
        return self.op("act", fn, deps)

    def dve(self, fn, deps=()):
        return self.op("dve", fn, deps)

    def pool(self, fn, deps=()):
        return self.op("pool", fn, deps)

    def emit(self, final_tokens):
        nc = self.nc
        sems = {}
        import contextlib
        with contextlib.ExitStack() as st:
            for k in self.sem_keys:
                sems[k] = st.enter_context(nc.semaphore(k))
            block = st.enter_context(nc.Block())

            def run(engname):
                def body(e):
                    for ws, fn, inc in self.q[engname]:
                        for (k, v) in ws:
                            e.wait_ge(sems[k], v)
                        ins = fn(e)
                        if inc is not None:
                            ins.then_inc(sems[inc[0]], inc[1])
                    if engname == "sp":
                        for (k, v) in final_tokens:
                            e.wait_ge(sems[k], v)
                return body

            block.tensor(run("pe"))
            block.scalar(run("act"))
            block.vector(run("dve"))
            block.gpsimd(run("pool"))
            block.sync(run("sp"))


class Ring:
    def __init__(self, P, name, tiles):
        self.P = P
        self.name = name
        self.tiles = tiles
        self.n = len(tiles)
        self.ready = [None] * self.n
        self.free = [None] * self.n

    def slot(self, i):
        return i % self.n

    def tile(self, i):
        return self.tiles[i % self.n]

    def load(self, i, eng, in_ap, out_fn=None):
        s = i % self.n
        out = self.tiles[s][:] if out_fn is None else out_fn(self.tiles[s])
        tok = self.P.dma(eng, out, in_ap, f"R_{self.name}_{s}", deps=[self.free[s]])
        self.ready[s] = tok
        self.free[s] = None
        return tok

    def rdy(self, i):
        return self.ready[i % self.n]

    def release(self, i, tok):
        s = i % self.n
        if self.free[s] is None:
            self.free[s] = []
        if not isinstance(self.free[s], list):
            self.free[s] = [self.free[s]]
        self.free[s].append(tok)


class Slots:
    def __init__(self, tiles):
        self.tiles = tiles
        self.n = len(tiles)
        self.free = [[] for _ in tiles]
        self.i = -1

    def next(self):
        self.i += 1
        s = self.i % self.n
        deps = self.free[s]
        self.free[s] = []
        return self.tiles[s], deps, s

    def release(self, s, tok):
        self.free[s].append(tok)


def _fm(v):
    v = np.asarray(v)
    return np.ascontiguousarray(v.reshape(-1, 128).T)


ATT_GROUPS = ((128, 1), (512, 4), (2048, 16))


def unit_tokens(g):
    d = ATT_GROUPS[g][1]
    L = S // d
    nb = L // 128
    tok = np.zeros((16, 128), dtype=np.int64)
    for r in range(d):
        for n in range(nb):
            tok[r * nb + n] = r + d * (128 * n + np.arange(128))
    return tok


def pack_shared(inp):
    sh = {}
    sh["w_ada"] = np.ascontiguousarray(inp["w_ada"], dtype=np.float32)
    sh["b_adaT"] = _fm(inp["b_ada"])
    nrm = np.stack([_fm(inp["norm_mix"][0]), _fm(inp["norm_mix"][1]),
                    _fm(inp["norm_ffn"][0]), _fm(inp["norm_ffn"][1]), _fm(inp["norm_out"])], axis=1)
    sh["nrm"] = np.ascontiguousarray(nrm)
    w = inp["m_w_in"][0]
    wk = w.reshape(16, 128, 6160)
    heads = []
    for h in range(8):
        cols = np.concatenate([np.arange(h * 128, (h + 1) * 128), 1024 + np.arange(h * 128, (h + 1) * 128),
                               2048 + np.arange(h * 256, (h + 1) * 256), 4096 + np.arange(h * 256, (h + 1) * 256)])
        heads.append(wk[:, :, cols].transpose(1, 0, 2))
    sh["m_wh"] = np.ascontiguousarray(np.stack(heads))
    sh["m_wg"] = np.ascontiguousarray(wk[:, :, 6144:6160].transpose(1, 0, 2))
    sh["m_gb"] = np.ascontiguousarray(np.broadcast_to(inp["m_gate_b"][0][None, None, :], (128, 16, 16)).reshape(128, 256))
    sh["m_hg"] = np.ascontiguousarray(np.broadcast_to(inp["m_head_norm"][0][None, :], (128, 2048)))
    sh["m_wo"] = _pack_out(inp["m_w_out"][0], 16)
    wa = inp["a_w_in"][0].reshape(16, 128, 3, 3, 16, 128)
    sh["a_wi"] = np.ascontiguousarray(wa.transpose(4, 2, 1, 0, 3, 5).reshape(16, 3, 128, 16, 384))
    sh["a_wo"] = _pack_out(inp["a_w_out"][0], 16)
    wu = inp["f_w_up"].reshape(2, 16, 128, 2, 44, 128)
    sh["f_wu"] = np.ascontiguousarray(wu.transpose(0, 4, 2, 1, 3, 5).reshape(2, 44, 128, 16, 256))
    cw = inp["f_conv_w"].reshape(2, 3, 88, 128)
    sh["f_cw"] = np.ascontiguousarray(cw.transpose(3, 0, 1, 2))
    cb = inp["f_conv_b"].reshape(2, 88, 128)
    sh["f_cb"] = np.ascontiguousarray(cb.transpose(2, 0, 1))
    sh["f_wd"] = np.ascontiguousarray(np.stack([_pack_out(inp["f_w_down"][l], 44) for l in range(2)]))
    return sh


def _pack_out(w, nk):
    return np.ascontiguousarray(w.reshape(nk, 128, 16, 128).transpose(2, 1, 0, 3))


def pack_core(inp, b):
    pc = {}
    pc["xT"] = np.ascontiguousarray(inp["x"][b].T.reshape(16, 128, S).transpose(1, 0, 2))
    pc["cT"] = _fm(inp["c"][b])
    pos = np.asarray(inp["positions"][b])
    pu = np.stack([pos[unit_tokens(g)].T for g in range(3)], axis=1)
    pc["posu"] = np.ascontiguousarray(pu.astype(np.int32))
    return pc


SHAPES = {
    "w_ada": ([2048, 24576], F32), "b_adaT": ([128, 192], F32), "nrm": ([128, 5, 16], F32),
    "m_wh": ([8, 128, 16, 768], F32), "m_wg": ([128, 16, 16], F32), "m_gb": ([128, 256], F32),
    "m_hg": ([128, 2048], F32), "m_wo": ([16, 128, 16, 128], F32),
    "a_wi": ([16, 3, 128, 16, 384], F32), "a_wo": ([16, 128, 16, 128], F32),
    "f_wu": ([2, 44, 128, 16, 256], F32), "f_cw": ([128, 2, 3, 88], F32), "f_cb": ([128, 2, 88], F32),
    "f_wd": ([2, 16, 128, 44, 128], F32),
    "xT": ([128, 16, S], F32), "cT": ([128, 16], F32), "posu": ([128, 3, 16], I32),
}


class Ctx:
    pass


_UID = [0]


def _uid():
    _UID[0] += 1
    return "u%d_" % _UID[0]


def _mm(out, lhsT, rhs, start, stop):
    return lambda e: e.matmul(out, lhsT=lhsT, rhs=rhs, start=start, stop=stop)


def build_program(upto="all", dbg=()):
    import contextlib
    nc = bass.Bass("TRN2", target_bir_lowering=False)
    P = Prog(nc)
    T = {}
    for k, (shp, dt_) in SHAPES.items():
        T[k] = nc.dram_tensor(k, shp, dt_, kind="ExternalInput").ap()
    T["outT"] = nc.dram_tensor("outT", [128, 16, S], F32, kind="ExternalOutput").ap()
    T["xa"] = nc.dram_tensor("xa", [128, 16, S], F32, kind="Internal").ap()
    T["xb"] = nc.dram_tensor("xb", [128, 16, S], F32, kind="Internal").ap()
    T["yTd"] = nc.dram_tensor("yTd", [128, 16, S], BF16, kind="Internal").ap()
    DBG = {}
    c = Ctx()
    c.nc, c.P, c.T, c.dbg, c.DBG = nc, P, T, dbg, DBG
    c.final = []

    with contextlib.ExitStack() as gst:
        def gsb(name, shape, dt_):
            return gst.enter_context(nc.sbuf_tensor("g_" + name, shape, dt_))
        c.banks = [gst.enter_context(nc.psum_tensor(f"bank{i}", [128, 512], F32)) for i in range(8)]
        c.modT = gsb("modT", [128, 192], F32)
        c.gm = gsb("gm", [128, 4, 16], F32)
        c.nrm = gsb("nrm", [128, 5, 16], F32)
        c.ones_m = gsb("ones_m", [128, 128], F32)
        c.ones32 = gsb("ones32", [128, 128], F32)
        c.onesb = gsb("onesb", [128, 128], BF16)
        c.identb = gsb("identb", [128, 128], BF16)
        c.tri = gsb("tri", [128, 128], F32)
        c.tri_ge = gsb("tri_ge", [128, 128], F32)
        c.eps_t = gsb("eps_t", [128, 1], F32)
        c.cs = gsb("cs", [128, 16], BF16)
        c.badaT = gsb("badaT", [128, 192], F32)
        c.one11 = gsb("one11", [1, 1], F32)
        c.hT = gsb("hT", [128, 16, S], BF16)
        consts_init(c)
        c.last = {}
        phase_ada(c)
        barrier(c)
        phase_norm(c, T["xT"], li=0, out_h=True)
        barrier(c)
        if "hT" in dbg:
            dbg_dump(c, "hT", c.hT, [128, 16, S], BF16)
        if upto == "norm0":
            return finish(c)
        phase_mlstm(c)
        barrier(c)
        if upto == "mlstm":
            dbg_dump(c, "yT", None, None, None) if False else None
            return finish(c)
        phase_outproj(c, T["m_wo"], 16, mod_col(c, 0, 2), T["xT"], T["xa"], next_li=2)
        barrier(c)
        if "xa" in dbg:
            dbg_dram(c, "xa", T["xa"], [128, 16, S], F32)
        if upto == "mix0":
            return finish(c)
        phase_ffn(c, 0, T["xa"], T["xb"], next_li=1)
        barrier(c)
        if "xb" in dbg:
            dbg_dram(c, "xb", T["xb"], [128, 16, S], F32)
        if upto == "ffn0":
            return finish(c)
        phase_attn(c)
        barrier(c)
        phase_outproj(c, T["a_wo"], 16, mod_col(c, 1, 2), T["xb"], T["xa"], next_li=3)
        barrier(c)
        if "xa2" in dbg:
            dbg_dram(c, "xa2", T["xa"], [128, 16, S], F32)
        if upto == "mix1":
            return finish(c)
        phase_ffn(c, 1, T["xa"], None, next_li=4, out_dram=T["outT"])
        return finish(c)


def finish(c):
    c.P.emit(c.final)
    return c.nc


def dbg_dump(c, name, sb_ap, shape, dt_):
    d = c.nc.dram_tensor("dbg_" + name, shape, dt_, kind="ExternalOutput").ap()
    tok = c.P.dma("sp", d, sb_ap[:] if hasattr(sb_ap, "shape") else sb_ap, "DBG_" + name, deps=c.bar)
    c.final.append(tok)
    c.DBG[name] = (shape, dt_)


def dbg_dram(c, name, src, shape, dt_):
    d = c.nc.dram_tensor("dbg_" + name, shape, dt_, kind="ExternalOutput").ap()
    tok = c.P.dma("sp", d, src, "DBG_" + name, deps=c.bar)
    c.final.append(tok)


def barrier(c):
    P = c.P
    toks = [(k, v) for k, v in P.cnt.items() if v > 0]
    c.bar = toks
    for e in P.ENG:
        P.op(e, _nop, deps=toks, sig=(e != "sp"))


def _nop(e):
    return e.nop()


def consts_init(c):
    P = c.P
    P.pool(lambda e: e.memset(c.ones_m[:], 1.0 / D))
    P.pool(lambda e: e.memset(c.eps_t[:], EPS))
    P.pool(lambda e: e.memset(c.ones32[:], 1.0))
    P.pool(lambda e: e.memset(c.onesb[:], 1.0))
    t1 = P.pool(lambda e: e.memset(c.tri[:], 1.0))
    t2 = P.pool(lambda e: e.memset(c.tri_ge[:], 1.0))
    t1 = P.pool(lambda e: e.affine_select(out=c.tri[:], in_=c.tri[:], pattern=[[1, 128]], compare_op=ALU.is_ge,
                                          fill=0.0, base=0, channel_multiplier=-1), deps=[t1])
    t2 = P.pool(lambda e: e.affine_select(out=c.tri_ge[:], in_=c.tri_ge[:], pattern=[[-1, 128]], compare_op=ALU.is_ge,
                                          fill=0.0, base=0, channel_multiplier=1), deps=[t2])
    c.tok_consts = P.pool(lambda e: e.tensor_tensor(out=c.identb[:], in0=c.tri[:], in1=c.tri_ge[:], op=ALU.mult),
                          deps=[t1, t2])
    c.tok_nrm = P.dma("sp", c.nrm[:], c.T["nrm"], "LD_nrm")


N_ADA_UP = 8


def phase_ada(c):
    import contextlib
    nc, P, T = c.nc, c.P, c.T
    with contextlib.ExitStack() as st:
        _u = _uid()

        def sb(name, shape, dt_):
            return st.enter_context(nc.sbuf_tensor(_u + name, shape, dt_))
        cs32 = sb("a_cs32", [128, 16], F32)
        cs, badaT, one11 = c.cs, c.badaT, c.one11
        mr = Slots([sb(f"a_modrow{i}", [1, 512], F32) for i in range(2)])
        wa = Ring(P, "wa", [sb(f"a_wa{i}", [128, 16, 512], BF16) for i in range(3)])
        t_c = P.dma("sp", cs32[:], T["cT"], "LD_c")
        c.t_bada = P.dma("sp", badaT[:], T["b_adaT"], "LD_bada")
        c.t_one = P.pool(lambda e: e.memset(one11[:], 1.0))
        c.t_cs = P.act(lambda e: e.activation(out=cs[:], in_=cs32[:], func=AF.Silu), deps=[t_c])
        t_b, t_one, t_cs = c.t_bada, c.t_one, c.t_cs
        wv = T["w_ada"].rearrange("(kc p) n -> p kc n", p=128)
        NT = N_ADA_UP
        ps = Slots([c.banks[0], c.banks[1]])
        pT = c.banks[2]
        for i in range(min(3, NT)):
            wa.load(i, "pool", wv[:, :, i * 512:(i + 1) * 512])
        for ct in range(NT):
            pt, pdeps, pslot = ps.next()
            w = wa.tile(ct)
            for hf in range(2):
                for kc in range(16):
                    tk = P.pe(_mm(pt[0:1, hf * 256:(hf + 1) * 256], cs[:, kc:kc + 1], w[:, kc, hf * 256:(hf + 1) * 256], kc == 0, kc == 15),
                              deps=[wa.rdy(ct), t_cs] + pdeps if (kc == 0 and hf == 0) else (), sig=(kc == 15 and hf == 1))
            wa.release(ct, tk)
            if ct + 3 < NT:
                wa.load(ct + 3, "pool", wv[:, :, (ct + 3) * 512:(ct + 4) * 512])
            mrow, mdeps, mslot = mr.next()
            t_ev = P.act(lambda e, pt=pt, mrow=mrow: e.activation(out=mrow[0:1, :], in_=pt[0:1, :], func=AF.Copy),
                         deps=[tk] + mdeps)
            ps.release(pslot, t_ev)
            for j in range(4):
                tk = P.pe(_mm(pT[:, ct * 4 + j:ct * 4 + j + 1], mrow[0:1, j * 128:(j + 1) * 128], one11[0:1, 0:1], True, True),
                          deps=[t_ev, t_one] if j == 0 else (), sig=(j == 3))
            mr.release(mslot, tk)
        nc_ = NT * 4
        t_mod = P.dve(lambda e: e.tensor_tensor(out=c.modT[:, 0:nc_], in0=pT[:, 0:nc_], in1=badaT[:, 0:nc_], op=ALU.add),
                      deps=[tk, t_b])
        P.dve(lambda e: e.scalar_tensor_tensor(out=c.gm[:, 0, :], in0=c.modT[:, 16:32], scalar=1.0,
                                               in1=c.nrm[:, 0, :], op0=ALU.add, op1=ALU.mult), deps=[t_mod, c.tok_nrm])


def ada_background(c, sb, bank, bank_fr, nring=3):
    P, T = c.P, c.T
    cs, one11 = c.cs, c.one11
    wv = T["w_ada"].rearrange("(kc p) n -> p kc n", p=128)
    c0 = N_ADA_UP * 512
    NTB = (24576 - c0) // 512
    wa = Ring(P, "wab", [sb(f"wab{i}", [128, 16, 512], BF16) for i in range(nring)])
    mr = Slots([sb(f"mrb{i}", [1, 512], F32) for i in range(2)])
    raw = sb("modraw", [128, 192], F32)
    for i in range(nring):
        wa.load(i, "pool", wv[:, :, c0 + i * 512:c0 + (i + 1) * 512])
    last = None
    pendT = None

    def transposes(ct, mrow, mslot, t_ev):
        for j in range(4):
            tk = P.pe(_mm(bank[:, j:j + 1], mrow[0:1, j * 128:(j + 1) * 128], one11[0:1, 0:1], True, True),
                      deps=[t_ev, c.t_one] + (bank_fr.take() if j == 0 else []) if j == 0 else (), sig=(j == 3))
        mr.release(mslot, tk)
        col = N_ADA_UP * 4 + ct * 4
        lt = P.act(lambda e, col=col: e.activation(out=raw[:, col:col + 4], in_=bank[:, 0:4], func=AF.Copy), deps=[tk])
        bank_fr.add(lt)
        return lt

    for ct in range(NTB):
        if pendT is not None:
            last = transposes(*pendT)
            pendT = None
        w = wa.tile(ct)
        for hf in range(2):
            for kc in range(16):
                tk = P.pe(_mm(bank[0:1, hf * 256:(hf + 1) * 256], cs[:, kc:kc + 1], w[:, kc, hf * 256:(hf + 1) * 256], kc == 0, kc == 15),
                          deps=[wa.rdy(ct), c.t_cs] + bank_fr.take() if (kc == 0 and hf == 0) else (), sig=(kc == 15 and hf == 1))
        wa.release(ct, tk)
        if ct + nring < NTB:
            wa.load(ct + nring, "pool", wv[:, :, c0 + (ct + nring) * 512:c0 + (ct + nring + 1) * 512])
        mrow, mdeps, mslot = mr.next()
        t_ev = P.act(lambda e, mrow=mrow: e.activation(out=mrow[0:1, :], in_=bank[0:1, 0:512], func=AF.Copy), deps=[tk] + mdeps)
        bank_fr.add(t_ev)
        pendT = (ct, mrow, mslot, t_ev)
        yield
    last = transposes(*pendT)
    n0 = N_ADA_UP * 4
    t_mod = P.dve(lambda e: e.tensor_tensor(out=c.modT[:, n0:192], in0=raw[:, n0:192], in1=c.badaT[:, n0:192], op=ALU.add),
                  deps=[last, c.t_bada])
    for (gi, jn, jm) in ((1, 2, 4), (2, 1, 7), (3, 3, 10)):
        P.dve(lambda e, gi=gi, jn=jn, jm=jm: e.scalar_tensor_tensor(
            out=c.gm[:, gi, :], in0=c.modT[:, jm * 16:(jm + 1) * 16], scalar=1.0,
            in1=c.nrm[:, jn, :], op0=ALU.add, op1=ALU.mult), deps=[t_mod, c.tok_nrm])
    yield


def mod_col(c, l, j):
    k = (l * 6 + j) * 16
    return c.modT[:, k:k + 16]


def phase_norm(c, x_src, li, out_h=True, out_dram=None):
    import contextlib
    nc, P, T = c.nc, c.P, c.T
    with contextlib.ExitStack() as st:
        _u = _uid()

        def sb(name, shape, dt_):
            return st.enter_context(nc.sbuf_tensor(_u + name, shape, dt_))
        xr = Ring(P, "nx", [sb(f"n_x{i}", [128, 16, 512], F32) for i in range(2)])
        sq = Slots([sb(f"n_sq{i}", [128, 512], F32) for i in range(3)])
        tmp = Slots([sb(f"n_t{i}", [128, 512], F32) for i in range(3)])
        rstd = Slots([sb(f"n_r{i}", [128, 512], F32) for i in range(2)])
        ot = None
        ps = Slots([c.banks[0], c.banks[1]])
        if li < 4:
            l, s_ = (li, 0) if li < 2 else (li - 2, 1)
            gvec = c.gm[:, l * 2 + s_, :]
            shv = mod_col(c, l, 0 if s_ == 0 else 3)
        else:
            gvec = c.nrm[:, 4, :]
            shv = None
        NTT = S // 512
        xr.load(0, "sp", x_src[:, :, 0:512])
        for tt in range(NTT):
            if tt + 1 < NTT:
                xr.load(tt + 1, "sp", x_src[:, :, (tt + 1) * 512:(tt + 2) * 512])
            xt = xr.tile(tt)
            pt, pdeps, pslot = ps.next()
            for fc in range(16):
                sqt, sdeps, sslot = sq.next()
                t_sq = P.act(lambda e, sqt=sqt, xt=xt, fc=fc: e.activation(out=sqt[:], in_=xt[:, fc, :], func=AF.Square),
                             deps=[xr.rdy(tt)] + sdeps)
                tk = P.pe(_mm(pt[:], c.ones_m[:], sqt[:], fc == 0, fc == 15),
                          deps=[t_sq] + (pdeps if fc == 0 else []), sig=True)
                sq.release(sslot, tk)
            rt, rdeps, rslot = rstd.next()
            t_r0 = P.act(lambda e, rt=rt, pt=pt: e.activation(out=rt[:], in_=pt[:], func=AF.Sqrt, bias=c.eps_t[:, 0:1], scale=1.0),
                         deps=[tk] + rdeps)
            ps.release(pslot, t_r0)
            t_r = P.dve(lambda e, rt=rt: e.reciprocal(out=rt[:], in_=rt[:]), deps=[t_r0])
            if not out_h:
                otile = xt
                odeps = []
                os_ = tt % 2
            for fc in range(16):
                if out_h:
                    tm, tdeps, tslot = tmp.next()
                    t_m = P.dve(lambda e, tm=tm, xt=xt, fc=fc, rt=rt: e.scalar_tensor_tensor(
                        out=tm[:], in0=xt[:, fc, :], scalar=gvec[:, fc:fc + 1], in1=rt[:], op0=ALU.mult, op1=ALU.mult),
                        deps=[t_r] + tdeps)
                    t_o = P.act(lambda e, tm=tm, fc=fc, tt=tt: e.activation(
                        out=c.hT[:, fc, tt * 512:(tt + 1) * 512], in_=tm[:], func=AF.Identity,
                        bias=shv[:, fc:fc + 1], scale=1.0), deps=[t_m])
                    tmp.release(tslot, t_o)
                    last = t_o
                else:
                    t_m = P.dve(lambda e, xt=xt, fc=fc, rt=rt, otile=otile: e.scalar_tensor_tensor(
                        out=otile[:, fc, :], in0=xt[:, fc, :], scalar=gvec[:, fc:fc + 1], in1=rt[:],
                        op0=ALU.mult, op1=ALU.mult), deps=[t_r] + (odeps if fc == 0 else []))
                    last = t_m
            rstd.release(rslot, last)
            if not out_h:
                tok = P.dma("sp", out_dram[:, :, tt * 512:(tt + 1) * 512], otile[:], f"ST_no_{os_}", deps=[last])
                xr.release(tt, tok)
                c.final.append(tok)
            else:
                xr.release(tt, last)


def phase_mlstm(c):
    import contextlib
    nc, P, T = c.nc, c.P, c.T
    B = c.banks
    with contextlib.ExitStack() as st:
        _u = _uid()

        def sb(name, shape, dt_):
            return st.enter_context(nc.sbuf_tensor(_u + "ml_" + name, shape, dt_))
        wg = sb("wg", [128, 16, 16], BF16)
        gb = sb("gb", [128, 256], F32)
        hg = sb("hg", [128, 2048], F32)
        t_wg = P.dma("pool", wg[:], T["m_wg"], "LD_mwg")
        t_gb = P.dma("sp", gb[:], T["m_gb"], "LD_mgb")
        t_hg = P.dma("sp", hg[:], T["m_hg"], "LD_mhg")
        wm = Ring(P, "wm", [sb(f"wm{i}", [128, 16, 768], BF16) for i in range(2)])
        wm.load(0, "pool", T["m_wh"][0])
        G = sb("G", [128, 16, 16], F32)
        Tn = sb("Tn", [128, 16, 16], F32)
        Lf = sb("Lf", [128, 16, 8], F32)
        tmpk = sb("tmpk", [128, 16, 8], F32)
        eq = sb("eq", [128, 16, 8], F32)
        ek = sb("ek", [128, 16, 8], F32)
        edec = sb("edec", [128, 16, 8], F32)
        lnq = sb("lnq", [128, 1], F32)
        t_lnq = P.pool(lambda e: e.memset(lnq[:], -0.5 * math.log(128.0)))
        gps = B[0]
        for tt in range(16):
            for kc in range(16):
                tk = P.pe(_mm(gps[:, tt * 16:(tt + 1) * 16], c.hT[:, kc, tt * 128:(tt + 1) * 128], wg[:, kc, :], kc == 0, kc == 15),
                          deps=[t_wg] if (tt == 0 and kc == 0) else (), sig=(tt == 15 and kc == 15))
        gps3 = gps[:, 0:256].rearrange("p (t g) -> p t g", g=16)
        gb3 = gb[:].rearrange("p (t g) -> p t g", g=16)
        t = P.dve(lambda e: e.tensor_tensor(out=G[:], in0=gps3, in1=gb3, op=ALU.add), deps=[tk, t_gb])
        t_tn = P.act(lambda e: e.activation(out=Tn[:], in_=G[:], func=AF.Tanh, scale=1.0 / 15.0), deps=[t])
        t = P.act(lambda e: e.activation(out=Lf[:], in_=Tn[:, :, 8:16], func=AF.Exp, scale=-15.0), deps=[t_tn])
        t_L = P.act(lambda e: e.activation(out=Lf[:], in_=Lf[:], func=AF.Ln, bias=1.0, scale=1.0), deps=[t])
        cps = B[1][:, 0:128].rearrange("p (t g) -> p t g", g=8)
        tps = B[1][:, 128:256].rearrange("p (t g) -> p t g", g=8)
        P.pe(lambda e: e.matmul(cps, lhsT=c.tri[:], rhs=Lf[:], start=True, stop=True), deps=[t_L, c.tok_consts, t])
        tk = P.pe(lambda e: e.matmul(tps, lhsT=c.ones32[:], rhs=Lf[:], start=True, stop=True), sig=True)
        t1 = P.act(lambda e: e.activation(out=eq[:], in_=cps, func=AF.Exp, scale=-1.0, bias=lnq[:, 0:1]), deps=[tk, t_lnq])
        t4 = P.act(lambda e: e.activation(out=edec[:], in_=tps, func=AF.Exp, scale=-1.0), deps=[tk])
        t2 = P.dve(lambda e: e.scalar_tensor_tensor(out=tmpk[:], in0=Tn[:, :, 0:8], scalar=15.0, in1=cps,
                                                    op0=ALU.mult, op1=ALU.add), deps=[tk, t_tn, t1, t4])
        t3 = P.act(lambda e: e.activation(out=ek[:], in_=tmpk[:], func=AF.Exp), deps=[t2])
        t_gates = [t1, t3, t4]
        if "gates" in c.dbg:
            c.bar = t_gates
            dbg_dump(c, "eq", eq, [128, 16, 8], F32)
            dbg_dump(c, "ek", ek, [128, 16, 8], F32)
            dbg_dump(c, "edec", edec, [128, 16, 8], F32)

        NB = 2
        qd = [sb(f"qd{i}", [128, 128], BF16) for i in range(NB)]
        kd = [sb(f"kd{i}", [128, 128], BF16) for i in range(NB)]
        V1 = [sb(f"V1{i}", [128, 260], BF16) for i in range(NB)]
        og = [sb(f"og{i}", [128, 256], F32) for i in range(NB)]
        qkT = [sb(f"qkT{i}", [128, 256], BF16) for i in range(NB)]
        WT = [sb(f"WT{i}", [128, 128], BF16) for i in range(NB)]
        junk = [sb(f"junk{i}", [128, 256], F32) for i in range(NB)]
        yb = [sb(f"yb{i}", [128, 256], BF16) for i in range(NB)]
        stt = [sb(f"st{i}", [128, 8], F32) for i in range(NB)]
        CT32 = sb("CT32", [128, 260], F32)
        CTb = [sb(f"CTb{i}", [128, 260], BF16) for i in range(2)]
        _ys0 = sb("ys0", [128, 2, S], BF16)
        ystage = [_ys0, _ys0]
        t_v1 = [P.pool(lambda e, i=i: e.memset(V1[i][:, 256:260], 1.0)) for i in range(NB)]

        class Fr:
            def __init__(self, init=()):
                self.t = list(init)

            def take(self):
                t = self.t
                self.t = []
                return t

            def add(self, *toks):
                self.t += [x for x in toks if x is not None]

        psQK = [B[0][:, 0:384], B[7][:, 0:384]]
        psVO = B[1]
        psQ = B[2][:, 0:256]
        psS = B[2][:, 256:384]
        psO = B[4][:, 0:257]
        psU = B[5][:, 0:257]
        psY = B[6][:, 0:256]
        fQK = [Fr(t_gates), Fr(t_gates)]
        fVO, fQ, fO, fU, fY = Fr(t_gates), Fr(), Fr(), Fr(), Fr()
        fS = fQ
        bg = ada_background(c, sb, B[3], Fr(t_gates))
        bg_state = {"done": False}

        def bg_step():
            if not bg_state["done"]:
                try:
                    next(bg)
                except StopIteration:
                    bg_state["done"] = True
        f_qd, f_kd, f_V1, f_og, f_qkT, f_WT, f_yb = ([Fr(), Fr()] for _ in range(7))
        f_CTb = [Fr(), Fr()]
        f_ct32 = Fr(t_gates)
        _fys = Fr()
        f_ys = [_fys, _fys]
        gate_dep = list(t_gates)
        A = {}

        def A1(h, tt, W, wtok):
            p = tt % 2
            pqk = psQK[p]
            for kc in range(16):
                tk = P.pe(_mm(pqk, c.hT[:, kc, tt * 128:(tt + 1) * 128], W[:, kc, 0:384], kc == 0, kc == 15),
                          deps=[wtok] + fQK[p].take() if kc == 0 else (), sig=(kc == 15))
            t_q = P.act(lambda e: e.activation(out=qd[p][:], in_=pqk[:, 0:128], func=AF.Copy, scale=eq[:, tt, h:h + 1]),
                        deps=[tk] + gate_dep + f_qd[p].take())
            t_k = P.act(lambda e: e.activation(out=kd[p][:], in_=pqk[:, 128:256], func=AF.Copy, scale=ek[:, tt, h:h + 1]),
                        deps=[tk] + gate_dep + f_kd[p].take())
            v1war = f_V1[p].take()
            t_va = P.act(lambda e: e.activation(out=V1[p][:, 0:128], in_=pqk[:, 256:384], func=AF.Copy),
                         deps=[tk, t_v1[p]] + v1war)
            fQK[p].add(t_q, t_k, t_va)
            A[tt] = dict(t_q=t_q, t_k=t_k, t_va=t_va, v1war=v1war)

        def A2(h, tt, W, wtok):
            p = tt % 2
            a = A[tt]
            for kc in range(16):
                tk = P.pe(_mm(psVO[:, 0:384], c.hT[:, kc, tt * 128:(tt + 1) * 128], W[:, kc, 384:768], kc == 0, kc == 15),
                          deps=[wtok] + fVO.take() if kc == 0 else (), sig=(kc == 15))
            t_vb = P.act(lambda e: e.activation(out=V1[p][:, 128:256], in_=psVO[:, 0:128], func=AF.Copy),
                         deps=[tk, t_v1[p]] + a["v1war"])
            t_v = [a["t_va"], t_vb]
            t_o = P.act(lambda e: e.activation(out=og[p][:], in_=psVO[:, 128:384], func=AF.Exp, scale=-1.0), deps=[tk] + f_og[p].take())
            fVO.add(t_vb, t_o)
            t_o1 = P.dve(lambda e: e.tensor_scalar(out=og[p][:], in0=og[p][:], scalar1=1.0, scalar2=None, op0=ALU.add), deps=[t_o])
            t_o2 = P.dve(lambda e: e.reciprocal(out=og[p][:], in_=og[p][:]), deps=[t_o1])
            t_og = P.pool(lambda e: e.tensor_tensor(out=og[p][:], in0=og[p][:], in1=hg[:, h * 256:(h + 1) * 256], op=ALU.mult),
                          deps=[t_o2, t_hg])
            P.pe(_mm(psQ[:, 0:128], qd[p][:], c.identb[:], True, True), deps=[a["t_q"], a["t_k"], c.tok_consts] + fQ.take())
            tk2 = P.pe(_mm(psQ[:, 128:256], kd[p][:], c.identb[:], True, True), sig=True)
            t_qk = P.dve(lambda e: e.tensor_copy(out=qkT[p][:], in_=psQ), deps=[tk2] + f_qkT[p].take())
            fQ.add(t_qk)
            f_qd[p].add(tk2)
            a.update(t_qk=t_qk, t_v=t_v, t_og=t_og)

        state = {"cur": 0, "t_ctb": None}

        def B1(h, tt):
            p = tt % 2
            a = A[tt]
            tk = P.pe(_mm(psS, qkT[p][:, 128:256], qkT[p][:, 0:128], True, True), deps=[a["t_qk"]] + fS.take(), sig=True)
            t_w = P.dve(lambda e: e.tensor_tensor(out=WT[p][:], in0=psS, in1=c.tri[:], op=ALU.mult),
                        deps=[tk, c.tok_consts] + f_WT[p].take())
            fS.add(t_w)
            a["t_w"] = t_w

        def B2(h, tt):
            p = tt % 2
            a = A[tt]
            cur = state["cur"]
            nxt = 1 - cur
            t_w = a["t_w"]
            P.pe(_mm(psO, WT[p][:], V1[p][:, 0:257], True, False), deps=[t_w, a["t_v"]] + fO.take())
            tk_o = P.pe(_mm(psO, qkT[p][:, 0:128], CTb[cur][:, 0:257], False, True), deps=[state["t_ctb"]], sig=True)
            f_WT[p].add(tk_o)
            f_CTb[cur].add(tk_o)
            tk_u = P.pe(_mm(psU, kd[p][:], V1[p][:, 0:257], True, True), deps=[a["t_k"], a["t_v"]] + fU.take(), sig=True)
            f_kd[p].add(tk_u)
            f_V1[p].add(tk_o, tk_u)
            f_qkT[p].add(tk_o)
            tp_ = max(tt - 1, 0)
            t_s2 = P.dve(lambda e: e.scalar_tensor_tensor(out=CT32[:, 0:257], in0=CT32[:, 0:257], scalar=edec[:, tp_, h:h + 1], in1=psU,
                                                          op0=ALU.mult, op1=ALU.add), deps=[tk_u] + f_ct32.take())
            fU.add(t_s2)
            t_cb = P.act(lambda e: e.activation(out=CTb[nxt][:, 0:257], in_=CT32[:, 0:257], func=AF.Copy, scale=edec[:, tt, h:h + 1]),
                         deps=[t_s2] + f_CTb[nxt].take())
            f_ct32.add(t_cb, t_s2)
            state["cur"] = nxt
            state["t_ctb"] = t_cb
            s = stt[p]
            t1_ = P.dve(lambda e: e.tensor_scalar(out=s[:, 0:1], in0=psO[:, 256:257], scalar1=-16.0, scalar2=16.0,
                                                  op0=ALU.mult, op1=ALU.max), deps=[tk_o] + f_yb[p].take())
            t2_ = P.dve(lambda e: e.scalar_tensor_tensor(out=s[:, 1:2], in0=psO[:, 256:257], scalar=16.0, in1=s[:, 0:1],
                                                         op0=ALU.mult, op1=ALU.max), deps=[t1_])
            t3_ = P.dve(lambda e: e.reciprocal(out=s[:, 2:3], in_=s[:, 1:2]), deps=[t2_])
            t4_ = P.act(lambda e: e.activation(out=junk[p][:], in_=psO[:, 0:256], func=AF.Square, scale=s[:, 2:3],
                                               accum_out=s[:, 3:4]), deps=[t3_, tk_o])
            t5_ = P.act(lambda e: e.activation(out=s[:, 4:5], in_=s[:, 3:4], func=AF.Ln, scale=1.0, bias=c.eps_t[:, 0:1]),
                        deps=[t4_])
            t6_ = P.act(lambda e: e.activation(out=s[:, 5:6], in_=s[:, 4:5], func=AF.Exp, scale=-0.5), deps=[t5_])
            t7_ = P.dve(lambda e: e.scalar_tensor_tensor(out=s[:, 6:7], in0=s[:, 2:3], scalar=16.0, in1=s[:, 5:6],
                                                         op0=ALU.mult, op1=ALU.mult), deps=[t6_, t3_])
            t_y = P.dve(lambda e: e.scalar_tensor_tensor(out=yb[p][:], in0=psO[:, 0:256], scalar=s[:, 6:7], in1=og[p][:],
                                                         op0=ALU.mult, op1=ALU.mult), deps=[t7_, a["t_og"], tk_o])
            fO.add(t_y)
            f_og[p].add(t_y)
            a["t_y"] = t_y

        def C(h, tt, ys, last_chunk_tokens):
            p = tt % 2
            a = A[tt]
            P.pe(_mm(psY[:, 0:128], yb[p][:, 0:128], c.identb[:], True, True), deps=[a["t_y"]] + fY.take())
            tk_y = P.pe(_mm(psY[:, 128:256], yb[p][:, 128:256], c.identb[:], True, True), sig=True)
            f_yb[p].add(tk_y)
            t_ys = P.act(lambda e: e.activation(out=ys[:, :, tt * 128:(tt + 1) * 128],
                                                in_=psY.rearrange("p (j s) -> p j s", j=2), func=AF.Copy),
                         deps=[tk_y] + (f_ys[h % 2].take() if tt == 0 else []))
            fY.add(t_ys)
            last_chunk_tokens.append(t_ys)

        for h in range(8):
            if h + 1 < 8:
                wm.load(h + 1, "pool", T["m_wh"][h + 1])
            W = wm.tile(h)
            wtok = wm.rdy(h)
            ys = ystage[h % 2]
            t_z1 = P.pool(lambda e: e.memset(CT32[:], 0.0), deps=f_ct32.take())
            t_z2 = P.pool(lambda e: e.memset(CTb[0][:], 0.0), deps=f_CTb[0].take() + f_CTb[1].take())
            f_ct32.add(t_z1)
            state["cur"] = 0
            state["t_ctb"] = t_z2
            lct = []
            for step in range(18):
                if step < 16:
                    A1(h, step, W, wtok)
                if 1 <= step <= 16:
                    B1(h, step - 1)
                if step < 16:
                    A2(h, step, W, wtok)
                if 1 <= step <= 16:
                    B2(h, step - 1)
                if step >= 2:
                    C(h, step - 2, ys, lct)
                if step < 16 and (h * 16 + step) % 3 == 1:
                    bg_step()
            wm.release(h, lct[-1])
            tok = P.dma("sp", T["yTd"][:, 2 * h:2 * h + 2, :], ys[:], f"ST_ys{h % 2}", deps=[lct[-1]])
            f_ys[h % 2].add(tok)
            c.last["yTd"] = c.last.get("yTd", []) + [tok]
        while not bg_state["done"]:
            bg_step()


def make_norm_epi(c, sb, bank, sq=None, rstd_slots=None, off_dve=False):
    P = c.P
    if sq is None:
        sq = Slots([sb("e_sq0", [128, 512], F32), sb("e_sq1", [128, 512], F32)])
    if rstd_slots is None:
        rstd_slots = Slots([sb("e_rstd", [128, 512], F32)])
    st_ = {"bank_free": []}

    def epi_steps(xnt, tq, li, deps_x, deps_mod, deps_h, out_dram=None, res=None):
        if li < 4:
            l, s_ = (li, 0) if li < 2 else (li - 2, 1)
            gvec = c.gm[:, l * 2 + s_, :]
            shv = mod_col(c, l, 0 if s_ == 0 else 3)
        else:
            gvec = c.nrm[:, 4, :]
            shv = None
        tk = None
        rstd, rdeps, rslot = rstd_slots.next()
        for fc in range(16):
            sqt, sdeps, sslot = sq.next()
            t_sq = P.act(lambda e, sqt=sqt, fc=fc: e.activation(out=sqt[:], in_=xnt[:, fc, :], func=AF.Square),
                         deps=list(deps_x) + sdeps)
            tk = P.pe(_mm(bank[:], c.ones_m[:], sqt[:], fc == 0, fc == 15),
                      deps=[t_sq] + (st_["bank_free"] if fc == 0 else []), sig=True)
            sq.release(sslot, tk)
            yield
        t_r0 = P.act(lambda e: e.activation(out=rstd[:], in_=bank[:], func=AF.Sqrt, bias=c.eps_t[:, 0:1], scale=1.0),
                     deps=[tk] + rdeps)
        st_["bank_free"] = [t_r0]
        t_r = P.dve(lambda e: e.reciprocal(out=rstd[:], in_=rstd[:]), deps=[t_r0])
        last = None
        for fc in range(16):
            if off_dve:
                t1 = P.pool(lambda e, fc=fc: e.tensor_tensor(out=xnt[:, fc, :], in0=xnt[:, fc, :], in1=rstd[:], op=ALU.mult),
                            deps=[t_r] + list(deps_mod))
                if li < 4:
                    last = P.act(lambda e, fc=fc: e.activation(out=c.hT[:, fc, tq * 512:(tq + 1) * 512], in_=xnt[:, fc, :],
                                                               func=AF.Identity, scale=gvec[:, fc:fc + 1], bias=shv[:, fc:fc + 1]),
                                 deps=[t1] + (list(deps_h) if fc == 0 else []))
                else:
                    last = P.act(lambda e, fc=fc: e.activation(out=xnt[:, fc, :], in_=xnt[:, fc, :], func=AF.Copy,
                                                               scale=gvec[:, fc:fc + 1]), deps=[t1])
                yield
                continue
            t1 = P.dve(lambda e, fc=fc: e.scalar_tensor_tensor(out=xnt[:, fc, :], in0=xnt[:, fc, :], scalar=gvec[:, fc:fc + 1],
                                                               in1=rstd[:], op0=ALU.mult, op1=ALU.mult),
                       deps=[t_r] + list(deps_mod))
            last = t1
            if li < 4:
                last = P.dve(lambda e, fc=fc: e.tensor_scalar(out=c.hT[:, fc, tq * 512:(tq + 1) * 512], in0=xnt[:, fc, :],
                                                              scalar1=shv[:, fc:fc + 1], scalar2=None, op0=ALU.add),
                             deps=[t1] + (list(deps_h) if fc == 0 else []))
            yield
        rstd_slots.release(rslot, last)
        if li == 4:
            tok = P.dma("sp", out_dram[:, :, tq * 512:(tq + 1) * 512], xnt[:], "ST_final", deps=[last])
            c.final.append(tok)
            last = tok
        c.last["hT_tiles"] = c.last.get("hT_tiles", []) + [last]
        if res is not None:
            res.append(last)

    def epi(xnt, tq, li, deps_x, deps_mod, deps_h, out_dram=None):
        res = []
        for _ in epi_steps(xnt, tq, li, deps_x, deps_mod, deps_h, out_dram=out_dram, res=res):
            pass
        return res[0]
    epi.steps = epi_steps
    return epi


def phase_outproj(c, w_dram, nk, gcol, x_src, x_dst, next_li):
    import contextlib
    nc, P, T = c.nc, c.P, c.T
    B = c.banks
    with contextlib.ExitStack() as st:
        _u = _uid()

        def sb(name, shape, dt_):
            return st.enter_context(nc.sbuf_tensor(_u + "o_" + name, shape, dt_))
        W = sb("w", [128, 16, nk, 128], BF16)
        t_w = [P.dma("pool", W[:, fo], w_dram[fo], f"LD_ow{fo % 4}") for fo in range(16)]
        t_src = [P.dma("sp", c.hT[:, :, tq * 512:(tq + 1) * 512], T["yTd"][:, :, tq * 512:(tq + 1) * 512], f"LD_osrc{tq}",
                       deps=c.last.get("yTd", [])) for tq in range(4)]
        xr = Ring(P, "ox", [sb(f"x{i}", [128, 512], F32) for i in range(2)])
        xnts = [sb(f"xnt{i}", [128, 16, 512], F32) for i in range(2)]
        epi = make_norm_epi(c, sb, B[6])
        ps = Slots([B[0], B[1], B[2], B[3]])
        seq = [(tq, fo) for tq in range(4) for fo in range(16)]
        for j in range(1):
            xr.load(j, "sp", x_src[:, seq[j][1], seq[j][0] * 512:(seq[j][0] + 1) * 512])
        xnt_free = [[], []]
        pend = None
        pend_res = []
        for tq in range(4):
            xnt = xnts[tq % 2]
            stores = []
            for fo in range(16):
                m = tq * 16 + fo
                if m + 1 < len(seq):
                    tq2, fo2 = seq[m + 1]
                    xr.load(m + 1, "sp", x_src[:, fo2, tq2 * 512:(tq2 + 1) * 512])
                pt, pdeps, pslot = ps.next()
                for kc in range(nk):
                    tk = P.pe(_mm(pt[:], W[:, fo, kc, :], c.hT[:, kc, tq * 512:(tq + 1) * 512], kc == 0, kc == nk - 1),
                              deps=[t_w[fo], t_src[tq]] + pdeps if kc == 0 else (), sig=(kc == nk - 1))
                xt = xr.tile(m)
                t_e = P.dve(lambda e, pt=pt, xt=xt, fo=fo, xnt=xnt: e.scalar_tensor_tensor(
                    out=xnt[:, fo, :], in0=pt[:], scalar=gcol[:, fo:fo + 1], in1=xt[:], op0=ALU.mult, op1=ALU.add),
                    deps=[tk, xr.rdy(m)] + (xnt_free[tq % 2] if fo == 0 else []))
                ps.release(pslot, t_e)
                xr.release(m, t_e)
                tok = P.dma("sp", x_dst[:, fo, tq * 512:(tq + 1) * 512], xnt[:, fo, :], f"ST_oxn{fo % 4}", deps=[t_e])
                stores.append(tok)
                c.last["x"] = c.last.get("x", []) + [tok]
                if fo >= 2 and pend is not None:
                    for _ in range(3):
                        try:
                            next(pend)
                        except StopIteration:
                            xnt_free[(tq - 1) % 2] = list(pend_res)
                            pend = None
                            break
            if pend is not None:
                for _ in pend:
                    pass
                xnt_free[(tq - 1) % 2] = list(pend_res)
                pend = None
            if tq == 3:
                xnt_free[tq % 2] = [epi(xnt, tq, next_li, [t_e], stores, [tk])]
            else:
                pend_res = []
                pend = epi.steps(xnt, tq, next_li, [t_e], list(stores), [tk], res=pend_res)


def phase_ffn(c, l, x_src, x_dst, next_li, out_dram=None):
    import contextlib
    nc, P, T = c.nc, c.P, c.T
    B = c.banks
    gcol = mod_col(c, l, 5)
    with contextlib.ExitStack() as st:
        _u = _uid()

        def sb(name, shape, dt_):
            return st.enter_context(nc.sbuf_tensor(_u + "ff_" + name, shape, dt_))
        wu = Ring(P, "fwu", [sb(f"wu{i}", [128, 16, 256], BF16) for i in range(3)])
        wd = Ring(P, "fwd", [sb(f"wd{i}", [128, 22, 128], BF16) for i in range(3)])
        act = sb("act", [128, 44, 512], BF16)
        cw = sb("cw", [128, 3, 88], F32)
        cb = sb("cb", [128, 88], F32)
        halo = [sb(f"halo{i}", [128, 88, 2], F32) for i in range(2)]
        tg = Slots([sb(f"tg{i}", [128, 512], F32) for i in range(2)])
        tv = Slots([sb(f"tv{i}", [128, 512], F32) for i in range(2)])
        sg = Slots([sb(f"sg{i}", [128, 512], F32) for i in range(2)])
        xr = Ring(P, "fx", [sb(f"x{i}", [128, 512], F32) for i in range(2)])
        e_rstd = Slots([sb("e_rstd", [128, 512], F32)])
        xnt = sb("xnt", [128, 16, 512], F32)
        epi = make_norm_epi(c, sb, B[6], sq=sg, rstd_slots=e_rstd, off_dve=True)
        xnt_free = []
        pend_epi = None
        pend_res = []
        t_cw = P.dma("sp", cw[:], T["f_cw"][:, l], "LD_fcw")
        t_cb = P.dma("sp", cb[:], T["f_cb"][:, l], "LD_fcb")
        psG = Slots([B[0], B[1]])
        psV = Slots([B[2], B[3]])
        psD = Slots([B[4], B[5]])
        act_free = []
        halo_w = [[None] * 88, [None] * 88]
        halo_r = [[None] * 88, [None] * 88]
        NP_ = 44
        nload = 0
        seq = [(tq, i) for tq in range(4) for i in range(NP_)]
        for j in range(3):
            wu.load(j, "pool", T["f_wu"][l, seq[j][1]])
        wd_seq = [(tq, fo) for tq in range(4) for fo in range(16)]
        def wd_src(m2):
            fo_ = wd_seq[m2 // 2][1]
            hf = m2 % 2
            return T["f_wd"][l, fo_][:, hf * 22:(hf + 1) * 22, :]
        for j in range(3):
            wd.load(j, "pool", wd_src(j))
        xr_i = 0
        xr.load(0, "sp", x_src[:, 0, 0:512])
        n_x = 2
        for tq in range(4):
            hp, hw = (tq - 1) % 2, tq % 2
            t_act = []
            for i in range(NP_):
                n = tq * NP_ + i
                W = wu.tile(n)
                pg, gdeps, gslot = psG.next()
                pv, vdeps, vslot = psV.next()
                for kc in range(16):
                    rhs = c.hT[:, kc, tq * 512:(tq + 1) * 512]
                    P.pe(_mm(pg[:], W[:, kc, 0:128], rhs, kc == 0, kc == 15),
                         deps=[wu.rdy(n)] + gdeps + vdeps if kc == 0 else ())
                    tk = P.pe(_mm(pv[:], W[:, kc, 128:256], rhs, kc == 0, kc == 15), sig=(kc == 15))
                wu.release(n, tk)
                t_up_last = tk
                if i >= 3 and pend_epi is not None:
                    try:
                        next(pend_epi)
                    except StopIteration:
                        xnt_free = list(pend_res)
                        pend_epi = None
                if n + 3 < len(seq):
                    wu.load(n + 3, "pool", T["f_wu"][l, seq[n + 3][1]])
                outs = []
                for (pp, ch, slots_, pslots, pslot) in ((pg, i, tg, psG, gslot), (pv, 44 + i, tv, psV, vslot)):
                    t_, tdeps, tslot = slots_.next()
                    t_e = P.act(lambda e, t_=t_, pp=pp, ch=ch: e.activation(out=t_[:], in_=pp[:], func=AF.Identity,
                                                                           scale=cw[:, 2, ch:ch + 1], bias=cb[:, ch:ch + 1]),
                                deps=[tk, t_cw, t_cb] + tdeps)
                    t_1 = P.dve(lambda e, t_=t_, pp=pp, ch=ch: e.scalar_tensor_tensor(
                        out=t_[:, 1:512], in0=pp[:, 0:511], scalar=cw[:, 1, ch:ch + 1], in1=t_[:, 1:512],
                        op0=ALU.mult, op1=ALU.add), deps=[t_e])
                    t_2 = P.dve(lambda e, t_=t_, pp=pp, ch=ch: e.scalar_tensor_tensor(
                        out=t_[:, 2:512], in0=pp[:, 0:510], scalar=cw[:, 0, ch:ch + 1], in1=t_[:, 2:512],
                        op0=ALU.mult, op1=ALU.add), deps=[t_1])
                    t_h = P.dve(lambda e, pp=pp, ch=ch, hw=hw: e.tensor_copy(out=halo[hw][:, ch, :], in_=pp[:, 510:512]),
                                deps=[t_2, halo_r[hw][ch]])
                    halo_w[hw][ch] = t_h
                    pslots.release(pslot, t_h)
                    last = t_2
                    if tq > 0:
                        t_3 = P.dve(lambda e, t_=t_, ch=ch, hp=hp: e.scalar_tensor_tensor(
                            out=t_[:, 0:1], in0=halo[hp][:, ch, 1:2], scalar=cw[:, 1, ch:ch + 1], in1=t_[:, 0:1],
                            op0=ALU.mult, op1=ALU.add), deps=[t_2, halo_w[hp][ch]])
                        t_4 = P.dve(lambda e, t_=t_, ch=ch, hp=hp: e.scalar_tensor_tensor(
                            out=t_[:, 0:2], in0=halo[hp][:, ch, 0:2], scalar=cw[:, 0, ch:ch + 1], in1=t_[:, 0:2],
                            op0=ALU.mult, op1=ALU.add), deps=[t_3])
                        halo_r[hp][ch] = t_4
                        last = t_4
                    outs.append((t_, last, slots_, tslot))
                (tgt, tg_last, _, tg_slot), (tvt, tv_last, _, tv_slot) = outs
                sgt, sdeps, sslot = sg.next()
                t_s = P.act(lambda e, sgt=sgt, tgt=tgt: e.activation(out=sgt[:], in_=tgt[:], func=AF.Silu),
                            deps=[tg_last] + sdeps)
                tg.release(tg_slot, t_s)
                t_m = P.pool(lambda e, sgt=sgt, tvt=tvt, i=i: e.tensor_tensor(out=act[:, i, :], in0=sgt[:], in1=tvt[:], op=ALU.mult),
                             deps=[t_s, tv_last] + (act_free if i == 0 else []))
                sg.release(sslot, t_m)
                tv.release(tv_slot, t_m)
                t_act.append(t_m)
            if pend_epi is not None:
                for _ in pend_epi:
                    pass
                xnt_free = list(pend_res)
                pend_epi = None
            act_free = []
            stores = []
            for fo in range(16):
                m = tq * 16 + fo
                pd, ddeps, dslot = psD.next()
                for hf in range(2):
                    m2 = 2 * m + hf
                    Wd = wd.tile(m2)
                    for c2 in range(22):
                        cc = hf * 22 + c2
                        tk = P.pe(_mm(pd[:], Wd[:, c2, :], act[:, cc, :], cc == 0, cc == NP_ - 1),
                                  deps=[t_act[cc]] + (([wd.rdy(m2)] + (ddeps if cc == 0 else [])) if c2 == 0 else []),
                                  sig=(c2 == 21))
                    wd.release(m2, tk)
                    if m2 + 3 < 2 * len(wd_seq):
                        wd.load(m2 + 3, "pool", wd_src(m2 + 3))
                xt = xr.tile(m)
                t_e = P.dve(lambda e, pd=pd, fo=fo, xt=xt: e.scalar_tensor_tensor(
                    out=xnt[:, fo, :], in0=pd[:], scalar=gcol[:, fo:fo + 1], in1=xt[:], op0=ALU.mult, op1=ALU.add),
                    deps=[tk, xr.rdy(m)] + (xnt_free if fo == 0 else []))
                psD.release(dslot, t_e)
                xr.release(m, t_e)
                if m + 1 < len(wd_seq):
                    tq2, fo2 = wd_seq[m + 1]
                    xr.load(m + 1, "sp", x_src[:, fo2, tq2 * 512:(tq2 + 1) * 512])
                if x_dst is not None:
                    tok = P.dma("sp", x_dst[:, fo, tq * 512:(tq + 1) * 512], xnt[:, fo, :], f"ST_fxo{fo % 4}", deps=[t_e])
                    stores.append(tok)
                    c.last["x"] = c.last.get("x", []) + [tok]
                if fo == 15:
                    act_free = [tk]
            if tq == 3:
                xnt_free = [epi(xnt, tq, next_li, [t_e], stores, [t_up_last], out_dram=out_dram)]
            else:
                pend_res = []
                pend_epi = epi.steps(xnt, tq, next_li, [t_e], list(stores), [t_up_last], out_dram=out_dram, res=pend_res)


ROPE_THETA = 500000.0


def phase_attn(c):
    import contextlib
    nc, P, T = c.nc, c.P, c.T
    B = c.banks

    class Fr:
        def __init__(self, init=()):
            self.t = list(init)

        def take(self):
            t = self.t
            self.t = []
            return t

        def add(self, *toks):
            self.t += [x for x in toks if x is not None]

    with contextlib.ExitStack() as st:
        _u = _uid()

        def sb(name, shape, dt_):
            return st.enter_context(nc.sbuf_tensor(_u + "at_" + name, shape, dt_))
        posi = sb("posi", [128, 48], I32)
        posf = sb("posf", [128, 48], F32)
        ang = sb("ang", [128, 48, 16], F32)
        angs = sb("angs", [128, 48, 16], F32)
        angc = sb("angc", [128, 48, 16], F32)
        sint = sb("sin", [128, 48, 16], F32)
        cost = sb("cos", [128, 48, 16], F32)
        mpi = sb("mpi", [128, 1], F32)
        maskT = sb("maskT", [128, 2, 128], F32)
        t_p = P.dma("sp", posi[:], T["posu"].rearrange("p g u -> p (g u)"), "LD_pos")
        t_mpi = P.pool(lambda e: e.memset(mpi[:], -math.pi))
        t_m1 = P.pool(lambda e: e.tensor_copy(out=maskT[:, 0, :], in_=c.tri[:]), deps=[c.tok_consts])
        t_m2 = P.pool(lambda e: e.tensor_copy(out=maskT[:, 1, :], in_=c.tri_ge[:]), deps=[c.tok_consts])
        t_pf = P.dve(lambda e: e.tensor_copy(out=posf[:], in_=posi[:]), deps=[t_p])
        t_a = None
        for j in range(16):
            invf = float(np.float32(ROPE_THETA) ** np.float32(-j / 16.0))
            t_a = P.dve(lambda e, j=j, invf=invf: e.tensor_scalar(out=ang[:, :, j], in0=posf[:], scalar1=invf, scalar2=None,
                                                                  op0=ALU.mult), deps=[t_pf])
        MAGIC = 12582912.0
        inv2pi = 1.0 / (2.0 * math.pi)
        qq = sb("qq", [128, 48, 16], F32)
        t_q = P.dve(lambda e: e.tensor_scalar(out=qq[:], in0=ang[:], scalar1=inv2pi, scalar2=None, op0=ALU.mult), deps=[t_a])
        outs_ = []
        for (dst, off, wk) in ((sint, 0.0, angs), (cost, 0.25, angc)):
            t0 = P.dve(lambda e, wk=wk, off=off: e.tensor_scalar(out=wk[:], in0=qq[:], scalar1=off, scalar2=None, op0=ALU.add), deps=[t_q])
            t1 = P.dve(lambda e, dst=dst, wk=wk: e.tensor_scalar(out=dst[:], in0=wk[:], scalar1=MAGIC, scalar2=None, op0=ALU.add), deps=[t0])
            t2 = P.dve(lambda e, dst=dst: e.tensor_scalar(out=dst[:], in0=dst[:], scalar1=-MAGIC, scalar2=None, op0=ALU.add), deps=[t1])
            t3 = P.dve(lambda e, dst=dst, wk=wk: e.tensor_tensor(out=wk[:], in0=wk[:], in1=dst[:], op=ALU.subtract), deps=[t2])
            t4 = P.act(lambda e, dst=dst, wk=wk: e.activation(out=dst[:], in_=wk[:], func=AF.Sin, scale=2.0 * math.pi), deps=[t3])
            outs_.append(t4)
        t_sin, t_cos = outs_
        t_rope = [t_sin, t_cos]
        if "rope" in c.dbg:
            c.bar = t_rope
            dbg_dump(c, "sin", sint, [128, 48, 16], F32)
            dbg_dump(c, "cos", cost, [128, 48, 16], F32)

        wa = Ring(P, "awi", [sb(f"wa{i}", [128, 16, 384], BF16) for i in range(2)])
        R = [sb(f"R{i}", [128, 2, 16, 32], F32) for i in range(2)]
        qkb = [sb(f"qkb{i}", [128, 2, 16, 128], BF16) for i in range(2)]
        Vb = [sb(f"Vb{i}", [128, 16, 128], BF16) for i in range(2)]
        qkT = sb("qkT", [128, 2, 16, 128], BF16)
        tA = sb("tA", [128, 16, 16], F32)
        tB = sb("tB", [128, 16, 16], F32)
        E = [sb(f"E{i}", [128, 2, 128], BF16) for i in range(4)]
        PT = [sb(f"PT{i}", [128, 2, 128], BF16) for i in range(4)]
        accN = sb("accN", [128, S], F32)
        accD = sb("accD", [128, S], F32)
        ost = [sb(f"ost{i}", [128, S], BF16) for i in range(2)]
        f_R, f_qkb, f_Vb = [Fr(), Fr()], [Fr(), Fr()], [Fr(), Fr()]
        f_qkT, f_tA, f_E, f_PT = Fr(), Fr(), [Fr() for _ in range(4)], [Fr() for _ in range(4)]
        f_acc = Fr()
        f_ost = [Fr(), Fr()]
        psP = [B[0], B[1]]
        f_psP = [Fr(), Fr()]
        psT = [B[2], B[3]]
        f_psT = [Fr(), Fr()]
        psS = [B[4], B[5]]
        f_psS = [Fr(), Fr()]
        psND = [B[6], B[7]]
        f_psND = [Fr(), Fr()]
        cnt = {"P": 0, "T": 0, "S": 0, "ND": 0, "E": 0}
        seq = [(h, g) for h in range(16) for g in range(3)]
        for j in range(2):
            wa.load(j, "pool", T["a_wi"][seq[j][0], seq[j][1]])
        Atok = {}

        def utok(g, u):
            d = ATT_GROUPS[g][1]
            nb = S // d // 128
            r, n = u // nb, u % nb
            start = r + d * 128 * n
            return slice(start, start + d * 127 + 1, d) if d > 1 else slice(start, start + 128)

        def projA(idx):
            h, g = seq[idx]
            b = idx % 2
            W = wa.tile(idx)
            evs = []
            for u in range(16):
                if u > 0:
                    yield
                s_ = cnt["P"] % 2
                cnt["P"] += 1
                pp = psP[s_]
                ts = utok(g, u)
                for kc in range(16):
                    tk = P.pe(_mm(pp[:, 0:384], c.hT[:, kc, ts], W[:, kc, :], kc == 0, kc == 15),
                              deps=[wa.rdy(idx)] + f_psP[s_].take() if kc == 0 else (), sig=(kc == 15))
                ppv = pp[:, 0:256].rearrange("p (a d) -> p a d", a=2)
                t1 = P.act(lambda e, ppv=ppv, u=u: e.activation(out=R[b][:, :, u, :], in_=ppv[:, :, 0:32], func=AF.Copy),
                           deps=[tk] + (f_R[b].take() if u == 0 else []))
                t2 = P.act(lambda e, ppv=ppv, u=u: e.activation(out=qkb[b][:, :, u, 32:128], in_=ppv[:, :, 32:128], func=AF.Copy),
                           deps=[tk] + (f_qkb[b].take() if u == 0 else []))
                t3 = P.dve(lambda e, pp=pp, u=u: e.tensor_copy(out=Vb[b][:, u, :], in_=pp[:, 256:384]),
                           deps=[tk, t1, t2] + (f_Vb[b].take() if u == 0 else []))
                f_psP[s_].add(t3)
                evs += [t1, t2, t3]
            wa.release(idx, tk)
            if idx + 2 < len(seq):
                wa.load(idx + 2, "pool", T["a_wi"][seq[idx + 2][0], seq[idx + 2][1]])
            Atok[idx] = evs[-3:]
            yield

        def pull(gen, n):
            if gen is None:
                return
            for _ in range(n):
                try:
                    next(gen)
                except StopIteration:
                    return

        Rtok = {}

        def ropeB(idx):
            h, g = seq[idx]
            b = idx % 2
            a = Atok[idx]
            last_rope = []
            for qk in range(2):
                t1v = R[b][:, qk, :, 0:16]
                t2v = R[b][:, qk, :, 16:32]
                cg = cost[:, g * 16:(g + 1) * 16, :]
                sg_ = sint[:, g * 16:(g + 1) * 16, :]
                ta = P.pool(lambda e, t1v=t1v, cg=cg: e.tensor_tensor(out=tA[:], in0=t1v, in1=cg, op=ALU.mult),
                            deps=a + t_rope + f_tA.take())
                tb = P.pool(lambda e, t2v=t2v, sg_=sg_: e.tensor_tensor(out=tB[:], in0=t2v, in1=sg_, op=ALU.mult), deps=[ta])
                to1 = P.pool(lambda e, qk=qk: e.tensor_tensor(out=qkb[b][:, qk, :, 0:16], in0=tA[:], in1=tB[:], op=ALU.subtract),
                             deps=[ta, tb])
                tc_ = P.pool(lambda e, t2v=t2v, cg=cg: e.tensor_tensor(out=tA[:], in0=t2v, in1=cg, op=ALU.mult), deps=[to1])
                td = P.pool(lambda e, t1v=t1v, sg_=sg_: e.tensor_tensor(out=tB[:], in0=t1v, in1=sg_, op=ALU.mult), deps=[to1])
                to2 = P.pool(lambda e, qk=qk: e.tensor_tensor(out=qkb[b][:, qk, :, 16:32], in0=tA[:], in1=tB[:], op=ALU.add),
                             deps=[tc_, td])
                f_tA.add(to2)
                last_rope = last_rope + [to1, to2]
            f_R[b].add(*last_rope)
            Rtok[idx] = last_rope

        def attnB(idx, gen):
            h, g = seq[idx]
            b = idx % 2
            d = ATT_GROUPS[g][1]
            nb = S // d // 128
            a = Atok[idx]
            last_rope = Rtok[idx]
            t_tr = []
            qkT_free = f_qkT.take()
            for qk in range(2):
                for u4 in range(4):
                    s_ = cnt["T"] % 2
                    cnt["T"] += 1
                    pt = psT[s_]
                    for uu in range(4):
                        u = u4 * 4 + uu
                        tk = P.pe(_mm(pt[:, uu * 128:(uu + 1) * 128], qkb[b][:, qk, u, :], c.identb[:], True, True),
                                  deps=last_rope + a + f_psT[s_].take() if uu == 0 else (), sig=(uu == 3))
                    if s_ == 0:
                        te = P.dve(lambda e, pt=pt, qk=qk, u4=u4: e.tensor_copy(
                            out=qkT[:, qk, u4 * 4:(u4 + 1) * 4, :], in_=pt[:, :].rearrange("p (u s) -> p u s", u=4)),
                            deps=[tk] + qkT_free)
                    else:
                        te = P.act(lambda e, pt=pt, qk=qk, u4=u4: e.activation(
                            out=qkT[:, qk, u4 * 4:(u4 + 1) * 4, :], in_=pt[:, :].rearrange("p (u s) -> p u s", u=4), func=AF.Copy),
                            deps=[tk] + qkT_free)
                    f_psT[s_].add(te)
                    t_tr.append(te)
                    if u4 % 2 == 1:
                        pull(gen, 1)
            f_qkb[b].add(tk)
            def s_stage(pj):
                pts = []
                for uu in range(2):
                    u = pj * 2 + uu
                    n = u % nb
                    blks = [u] + ([u - 1] if n > 0 else [])
                    s_s = cnt["S"] % 2
                    cnt["S"] += 1
                    ps_ = psS[s_s]
                    for bi, ub in enumerate(blks):
                        tk = P.pe(_mm(ps_[:, bi * 128:(bi + 1) * 128], qkT[:, 1, ub, :], qkT[:, 0, u, :], True, True),
                                  deps=t_tr + f_psS[s_s].take() if bi == 0 else (), sig=(bi == len(blks) - 1))
                    nbk = len(blks)
                    eb = cnt["E"] % 4
                    cnt["E"] += 1
                    te = P.act(lambda e, ps_=ps_, eb=eb, nbk=nbk: e.activation(
                        out=E[eb][:, 0:nbk, :], in_=ps_[:, 0:nbk * 128].rearrange("p (a s) -> p a s", a=nbk), func=AF.Exp,
                        scale=128.0 ** -0.5), deps=[tk] + f_E[eb].take())
                    f_psS[s_s].add(te)
                    tm = P.dve(lambda e, eb=eb, nbk=nbk: e.tensor_tensor(out=PT[eb][:, 0:nbk, :], in0=E[eb][:, 0:nbk, :],
                                                                          in1=maskT[:, 0:nbk, :], op=ALU.mult),
                                deps=[te, t_m1, t_m2] + f_PT[eb].take())
                    f_E[eb].add(tm)
                    pts.append((eb, blks, tm))
                return pts

            def av_stage(pj, pts):
                s_nd = cnt["ND"] % 2
                cnt["ND"] += 1
                pnd = psND[s_nd]
                first_nd = True
                tk_nd = None
                for uu, (eb, blks, tm) in enumerate(pts):
                    for bi, ub in enumerate(blks):
                        P.pe(_mm(pnd[:, uu * 128:(uu + 1) * 128], Vb[b][:, ub, :], PT[eb][:, bi, :], bi == 0, bi == len(blks) - 1),
                             deps=[tm, a[2]] + (f_psND[s_nd].take() if first_nd else []))
                        first_nd = False
                    for bi, ub in enumerate(blks):
                        tk_nd = P.pe(_mm(pnd[:, 256 + uu * 128:256 + (uu + 1) * 128], c.onesb[:], PT[eb][:, bi, :],
                                         bi == 0, bi == len(blks) - 1), sig=(bi == len(blks) - 1))
                    f_PT[eb].add(tk_nd)
                if g == 0:
                    vN, vD = accN[:, pj * 256:(pj + 1) * 256], accD[:, pj * 256:(pj + 1) * 256]
                    pN, pD = pnd[:, 0:256], pnd[:, 256:512]
                elif g == 1:
                    r, n0 = pj // 2, (pj % 2) * 2
                    st_ = r + 512 * n0
                    vN, vD = accN[:, st_:st_ + 1021:4], accD[:, st_:st_ + 1021:4]
                    pN, pD = pnd[:, 0:256], pnd[:, 256:512]
                else:
                    r0 = 2 * pj
                    vN = accN[:].rearrange("p (i r) -> p r i", r=16)[:, r0:r0 + 2, :]
                    vD = accD[:].rearrange("p (i r) -> p r i", r=16)[:, r0:r0 + 2, :]
                    pN = pnd[:, 0:256].rearrange("p (a i) -> p a i", a=2)
                    pD = pnd[:, 256:512].rearrange("p (a i) -> p a i", a=2)
                if g == 0:
                    t1 = P.dve(lambda e, vN=vN, pN=pN: e.tensor_copy(out=vN, in_=pN), deps=[tk_nd] + (f_acc.take() if pj == 0 else []))
                    t2 = P.dve(lambda e, vD=vD, pD=pD: e.tensor_copy(out=vD, in_=pD), deps=[tk_nd])
                else:
                    t1 = P.dve(lambda e, vN=vN, pN=pN: e.tensor_tensor(out=vN, in0=pN, in1=vN, op=ALU.add), deps=[tk_nd, Atok["acc"]])
                    t2 = P.dve(lambda e, vD=vD, pD=pD: e.tensor_tensor(out=vD, in0=pD, in1=vD, op=ALU.add), deps=[tk_nd, Atok["acc"]])
                f_psND[s_nd].add(t1, t2)
                Atok["acc_last"] = t2
                return tk_nd

            nxt_pts = s_stage(0)
            for pj in range(8):
                cur_pts = nxt_pts
                if pj + 1 < 8:
                    nxt_pts = s_stage(pj + 1)
                pull(gen, 1)
                tk_nd = av_stage(pj, cur_pts)
            Atok["acc"] = Atok["acc_last"]
            f_Vb[b].add(tk_nd)
            f_qkT.add(tk_nd)
            if g == 2:
                ob = h % 2
                tr = P.dve(lambda e: e.reciprocal(out=accD[:], in_=accD[:]), deps=[Atok["acc"]])
                to = P.dve(lambda e, ob=ob: e.tensor_tensor(out=ost[ob][:], in0=accN[:], in1=accD[:], op=ALU.mult),
                           deps=[tr] + f_ost[ob].take())
                f_acc.add(to)
                tok = P.dma("sp", T["yTd"][:, h, :], ost[ob][:], f"ST_ao{ob}", deps=[to] + (c.last.get("src_loaded", []) if h < 2 else []))
                f_ost[ob].add(tok)
                c.last["yTd"] = c.last.get("yTd", []) + [tok]

        c.last["yTd"] = []
        for idx in range(len(seq) + 1):
            gen = projA(idx) if idx < len(seq) else None
            if idx >= 1:
                ropeB(idx - 1)
            pull(gen, 4)
            if idx >= 1:
                attnB(idx - 1, gen)
            pull(gen, 20)


_PROG_CACHE = {}


def kernel(**inputs):
    inp = {k: np.asarray(v) for k, v in inputs.items()}
    if "nc" not in _PROG_CACHE:
        _PROG_CACHE["nc"] = build_program()
    nc = _PROG_CACHE["nc"]
    sh = pack_shared(inp)
    in_maps = [dict(sh, **pack_core(inp, b)) for b in range(N_CORES)]
    res = run_bass_kernel_spmd(nc, in_maps, core_ids=list(range(N_CORES)))
    out = np.empty((N_CORES, S, D), dtype=np.float32)
    for b in range(N_CORES):
        oT = np.asarray(res.results[b]["outT"])
        out[b] = oT.transpose(2, 1, 0).reshape(S, D)
    return out
```
